# Optimizing a Trainium2 kernel written in Bass

```python
import jax, jax.numpy as jnp
from jax import lax
import numpy as np

D_MODEL = 1024
BATCH = 8
SEQ = 2048
DEPTH = 2

CHUNK = 64
N_META = 16
META_PAD = CHUNK - N_META
HEAD_DIM = 64
A_HEADS = 8
IDX_HEADS = 8
IDX_DIM = 64
DSA_MAX_K = 256
A_QBLK = 64
B_HEADS = 8
B_VDIM = 2 * HEAD_DIM
C_HEADS = 8
C_KV_HEADS = 2
C_WINDOW = 128
C_WIN_CHUNKS = C_WINDOW // CHUNK
D_WIDTH = 512
POOL_WINDOWS = (2, 4, 8, 16)
D_GROUP = D_WIDTH // len(POOL_WINDOWS)
D_FF = 2816
EPS = 1e-6
NEG = -1e30

EVEN_SIZES = ([A_HEADS * HEAD_DIM] * 3 + [IDX_HEADS * IDX_DIM, IDX_DIM, IDX_HEADS]
              + [B_HEADS * HEAD_DIM] * 2 + [B_HEADS * B_VDIM] * 2)
EVEN_IN = sum(EVEN_SIZES)
EVEN_OUT = A_HEADS * HEAD_DIM + B_HEADS * B_VDIM
ODD_SIZES = [C_HEADS * HEAD_DIM, C_KV_HEADS * HEAD_DIM, C_KV_HEADS * HEAD_DIM, D_WIDTH]
ODD_IN = sum(ODD_SIZES)
ODD_OUT = C_HEADS * HEAD_DIM + D_WIDTH
N_EVEN = (DEPTH + 1) // 2
N_ODD = DEPTH // 2

kernel_name = "hybrid_streaming_dsa_retention_swa_pool"


def rms_norm(x, gain=None):
    xf = x.astype(jnp.float32)
    y = xf * lax.rsqrt(jnp.mean(xf * xf, axis=-1, keepdims=True) + EPS)
    if gain is not None:
        y = y * gain.astype(jnp.float32)
    return y.astype(x.dtype)


def swiglu(x, w_in, w_out):
    g, u = jnp.split(x @ w_in, 2, axis=-1)
    return (jax.nn.silu(g) * u) @ w_out


def split_cols(a, sizes):
    out, off = [], 0
    for s in sizes:
        out.append(a[..., off:off + s])
        off += s
    return out


def chunk_ids(n):
    return (jnp.arange(n) + META_PAD) // CHUNK


def pad_left(a):
    return jnp.pad(a, [(0, 0), (META_PAD, 0)] + [(0, 0)] * (a.ndim - 2))


def dsa_attention(q, k, v, iq, ik, iw, top_k):
    B, L, H, dh = q.shape
    cid = chunk_ids(L)
    nb = -(-L // A_QBLK)
    padn = nb * A_QBLK - L

    def blocks(a):
        a = jnp.pad(a, [(0, 0), (0, padn)] + [(0, 0)] * (a.ndim - 2))
        return jnp.moveaxis(a.reshape((B, nb, A_QBLK) + a.shape[2:]), 1, 0)

    q_cid = chunk_ids(nb * A_QBLK).reshape(nb, A_QBLK)
    iw = iw * (IDX_HEADS ** -0.5 * IDX_DIM ** -0.5)
    gather = jax.vmap(lambda t, i: t[i])

    def one_block(args):
        qb, iqb, iwb, qc = args
        logits = jnp.einsum('bqhd,bsd->bqhs', iqb, ik)
        score = jnp.einsum('bqh,bqhs->bqs', iwb, jax.nn.relu(logits)).astype(jnp.float32)
        adm = cid[None, :] <= qc[:, None]
        score = jnp.where(adm[None], score, NEG)
        _, idx = lax.top_k(score, top_k)
        valid = cid[idx] <= qc[None, :, None]
        kg = gather(k, idx)
        vg = gather(v, idx)
        s = jnp.einsum('bqhd,bqkhd->bhqk', qb, kg).astype(jnp.float32) * dh ** -0.5
        s = jnp.where(valid[:, None], s, NEG)
        p = jax.nn.softmax(s, axis=-1).astype(v.dtype)
        return jnp.einsum('bhqk,bqkhd->bqhd', p, vg)

    out = lax.map(one_block, (blocks(q), blocks(iq), blocks(iw), q_cid))
    return jnp.moveaxis(out, 0, 1).reshape(B, nb * A_QBLK, H, dh)[:, :L]


def rotary(x, pos):
    d = x.shape[-1]
    inv = 1.0 / (10000.0 ** (jnp.arange(0, d, 2, dtype=jnp.float32) / d))
    ang = pos.astype(jnp.float32)[:, None] * inv[None]
    cos = jnp.cos(ang)[None, :, None, :]
    sin = jnp.sin(ang)[None, :, None, :]
    x1, x2 = jnp.split(x.astype(jnp.float32), 2, axis=-1)
    return jnp.concatenate([x1 * cos - x2 * sin, x1 * sin + x2 * cos], axis=-1).astype(x.dtype)


def retention(q, k, v):
    B, Lp, H, dk = q.shape
    dv = v.shape[-1]
    N = Lp // CHUNK
    pos = jnp.arange(Lp) - META_PAD
    q = rotary(q, pos)
    k = rotary(k, pos) * dk ** -0.5
    log_g = jnp.log(1.0 - 2.0 ** (-5.0 - jnp.arange(H, dtype=jnp.float32)))
    i = jnp.arange(CHUNK, dtype=jnp.float32)
    rel = i[:, None] - i[None, :]
    dmat = jnp.where(rel >= 0, jnp.exp(jnp.maximum(rel, 0.0)[None] * log_g[:, None, None]), 0.0)
    q_dec = jnp.exp((i + 1.0)[:, None] * log_g[None])
    k_dec = jnp.exp((CHUNK - 1.0 - i)[:, None] * log_g[None])
    c_dec = jnp.exp(CHUNK * log_g)
    qc = q.reshape(B, N, CHUNK, H, dk)
    kc = k.reshape(B, N, CHUNK, H, dk)
    vc = v.reshape(B, N, CHUNK, H, dv)
    inner = jnp.einsum('bnihd,bnjhd->bnhij', qc, kc) * dmat.astype(q.dtype)
    inner = jnp.einsum('bnhij,bnjhe->bnihe', inner, vc)

    def step(S, xs):
        qn, kn, vn = xs
        cross = jnp.einsum('bihd,bhde->bihe', qn, S) * q_dec[None, :, :, None]
        S = S * c_dec[None, :, None, None] + jnp.einsum(
            'bjhd,bjhe->bhde', kn * k_dec[None, :, :, None], vn)
        return S, cross

    S0 = jnp.zeros((B, H, dk, dv), jnp.float32)
    _, cross = lax.scan(step, S0, (jnp.moveaxis(qc, 1, 0), jnp.moveaxis(kc, 1, 0),
                                   jnp.moveaxis(vc, 1, 0)))
    out = inner + jnp.moveaxis(cross, 0, 1).astype(inner.dtype)
    return out.reshape(B, Lp, H, dv)


def swa_sinks(q, k, v, sinks):
    B, Lp, H, dh = q.shape
    G = k.shape[2]
    R = H // G
    N = Lp // CHUNK
    W = C_WIN_CHUNKS

    def band(t):
        tc = t.reshape(B, N, CHUNK, G, dh)
        tp = jnp.pad(tc, ((0, 0), (W, 0), (0, 0), (0, 0), (0, 0)))
        bnd = jnp.concatenate([tp[:, j:j + N] for j in range(W + 1)], axis=2)
        meta = jnp.broadcast_to(t[:, META_PAD:CHUNK][:, None], (B, N, N_META, G, dh))
        return jnp.concatenate([meta, bnd], axis=2)

    kb, vb = band(k), band(v)
    key_chunk = jnp.arange(N)[:, None] - W + (jnp.arange((W + 1) * CHUNK) // CHUNK)[None]
    valid = jnp.concatenate([jnp.ones((N, N_META), bool), key_chunk >= 1], axis=1)
    qg = q.reshape(B, N, CHUNK, G, R, dh)
    s = jnp.einsum('bncgrd,bnkgd->bngrck', qg, kb).astype(jnp.float32) * dh ** -0.5
    s = jnp.where(valid[None, :, None, None, None, :], s, NEG)
    sink = jnp.broadcast_to(sinks.reshape(G, R).astype(jnp.float32)[None, None, :, :, None, None],
                            s.shape[:-1] + (1,))
    p = jax.nn.softmax(jnp.concatenate([s, sink], axis=-1), axis=-1)[..., :-1].astype(v.dtype)
    out = jnp.einsum('bngrck,bnkgd->bncgrd', p, vb)
    return out.reshape(B, Lp, H, dh)


def pool_mixer(x, d_mix, d_scale):
    B, L, _ = x.shape
    xg = x.astype(jnp.float32).reshape(B, L, len(POOL_WINDOWS), D_GROUP)
    cs = jnp.pad(jnp.cumsum(xg, axis=1), ((0, 0), (1, 0), (0, 0), (0, 0)))
    t = jnp.arange(L)
    pooled = []
    for gi, w in enumerate(POOL_WINDOWS):
        c = cs[:, :, gi]
        lag = jnp.pad(c, ((0, 0), (w, 0), (0, 0)))[:, 1:L + 1]
        cnt = jnp.minimum(t + 1, w).astype(jnp.float32)[None, :, None]
        pooled.append((c[:, 1:] - lag) / cnt)
    pooled = jnp.stack(pooled, axis=2)
    y = (pooled - xg).astype(x.dtype)
    y = jnp.einsum('blgc,gce->blge', y, d_mix).reshape(B, L, D_WIDTH)
    return y * d_scale


def even_mixer(h, w_in, a_qn, a_kn, w_out, top_k):
    B, L, _ = h.shape
    aq, ak, av, iq, ik, iw, bq, bk, bv, bg = split_cols(h @ w_in, EVEN_SIZES)
    hd = lambda t, n: t.reshape(B, L, n, -1)
    aq = rms_norm(hd(aq, A_HEADS), a_qn)
    ak = rms_norm(hd(ak, A_HEADS), a_kn)
    ya = dsa_attention(aq, ak, hd(av, A_HEADS), hd(iq, IDX_HEADS), ik, iw, top_k)
    ya = ya.reshape(B, L, -1)
    yb = retention(pad_left(hd(bq, B_HEADS)), pad_left(hd(bk, B_HEADS)),
                   pad_left(hd(bv, B_HEADS)))[:, META_PAD:]
    yb = rms_norm(yb).astype(h.dtype).reshape(B, L, -1) * jax.nn.silu(bg)
    return jnp.concatenate([ya, yb], axis=-1) @ w_out


def odd_mixer(h, w_in, c_qn, c_kn, c_sinks, d_mix, d_scale, w_out):
    B, L, _ = h.shape
    cq, ck, cv, dx = split_cols(h @ w_in, ODD_SIZES)
    hd = lambda t, n: t.reshape(B, L, n, -1)
    cq = rms_norm(hd(cq, C_HEADS), c_qn)
    ck = rms_norm(hd(ck, C_KV_HEADS), c_kn)
    yc = swa_sinks(pad_left(cq), pad_left(ck), pad_left(hd(cv, C_KV_HEADS)), c_sinks)
    yc = yc[:, META_PAD:].reshape(B, L, -1)
    yd = pool_mixer(dx, d_mix, d_scale)
    return jnp.concatenate([yc, yd], axis=-1) @ w_out


def setup_inputs(seed: int = 0) -> dict:
    key = jax.random.key(seed)
    ks = jax.random.split(key, 21)
    f32 = jnp.float32
    nrm = lambda k, shape, fan: jax.random.normal(k, shape, f32) * fan ** -0.5
    gain = lambda k, shape: 1.0 + 0.05 * jax.random.normal(k, shape, f32)
    return {
        "x": jax.random.normal(ks[0], (BATCH, SEQ, D_MODEL), f32),
        "meta_tokens": jax.random.normal(ks[1], (N_META, D_MODEL), f32),
        "ffn1_norm": gain(ks[2], (DEPTH, D_MODEL)),
        "ffn1_w_in": nrm(ks[3], (DEPTH, D_MODEL, 2 * D_FF), D_MODEL),
        "ffn1_w_out": nrm(ks[4], (DEPTH, D_FF, D_MODEL), D_FF),
        "mix_norm": gain(ks[5], (DEPTH, D_MODEL)),
        "ffn2_norm": gain(ks[6], (DEPTH, D_MODEL)),
        "ffn2_w_in": nrm(ks[7], (DEPTH, D_MODEL, 2 * D_FF), D_MODEL),
        "ffn2_w_out": nrm(ks[8], (DEPTH, D_FF, D_MODEL), D_FF),
        "ev_w_in": nrm(ks[9], (N_EVEN, D_MODEL, EVEN_IN), D_MODEL),
        "ev_a_q_norm": gain(ks[10], (N_EVEN, HEAD_DIM)),
        "ev_a_k_norm": gain(ks[11], (N_EVEN, HEAD_DIM)),
        "ev_w_out": nrm(ks[12], (N_EVEN, EVEN_OUT, D_MODEL), EVEN_OUT),
        "od_w_in": nrm(ks[13], (N_ODD, D_MODEL, ODD_IN), D_MODEL),
        "od_c_q_norm": gain(ks[14], (N_ODD, HEAD_DIM)),
        "od_c_k_norm": gain(ks[15], (N_ODD, HEAD_DIM)),
        "od_c_sinks": 0.5 * jax.random.normal(ks[16], (N_ODD, C_HEADS), f32),
        "od_d_mix": nrm(ks[17], (N_ODD, len(POOL_WINDOWS), D_GROUP, D_GROUP), D_GROUP),
        "od_d_scale": gain(ks[18], (N_ODD, D_WIDTH)),
        "od_w_out": nrm(ks[19], (N_ODD, ODD_OUT, D_MODEL), ODD_OUT),
    }


def reference(x, meta_tokens, ffn1_norm, ffn1_w_in, ffn1_w_out, mix_norm, ffn2_norm,
              ffn2_w_in, ffn2_w_out, ev_w_in, ev_a_q_norm, ev_a_k_norm, ev_w_out,
              od_w_in, od_c_q_norm, od_c_k_norm, od_c_sinks, od_d_mix, od_d_scale,
              od_w_out):
    B, S, D = x.shape
    top_k = min(DSA_MAX_K, S // 4)
    h = jnp.concatenate([jnp.broadcast_to(meta_tokens.astype(x.dtype)[None], (B, N_META, D)), x],
                        axis=1)
    for layer in range(DEPTH):
        h = h + (0.5 * swiglu(rms_norm(h, ffn1_norm[layer]), ffn1_w_in[layer],
                              ffn1_w_out[layer])).astype(h.dtype)
        hn = rms_norm(h, mix_norm[layer])
        if layer % 2 == 0:
            e = layer // 2
            y = even_mixer(hn, ev_w_in[e], ev_a_q_norm[e], ev_a_k_norm[e], ev_w_out[e], top_k)
        else:
            o = layer // 2
            y = odd_mixer(hn, od_w_in[o], od_c_q_norm[o], od_c_k_norm[o], od_c_sinks[o],
                          od_d_mix[o], od_d_scale[o], od_w_out[o])
        h = h + y.astype(h.dtype)
        h = h + (0.5 * swiglu(rms_norm(h, ffn2_norm[layer]), ffn2_w_in[layer],
                              ffn2_w_out[layer])).astype(h.dtype)
    return h[:, N_META:]
```

```python
import contextlib
import numpy as np
import concourse.bass as bass
import concourse.mybir as mybir
from concourse.bass_utils import run_bass_kernel_spmd

F32 = mybir.dt.float32
BF16 = mybir.dt.bfloat16
AF = mybir.ActivationFunctionType
ALU = mybir.AluOpType
AX = mybir.AxisListType

ENGS = ["pe", "act", "dve", "pool", "sp"]
SAME_ENG_SYNC = True

D = 1024
KC = 8
S = 2048
NM = 16
T = S + NM
DFF = 2816
FC = 22
EPS = 1e-6
BLKS = [(0, 16), (16, 528), (528, 1040), (1040, 1552), (1552, 2064)]
HALVES = [[0, 1, 2], [3, 4]]
NRING = 3
RING_ELEMS = 6144


class Prog:
    def __init__(self, nc):
        self.nc = nc
        self.plan = False
        self.reset()

    def reset(self):
        self.q = {e: [] for e in ENGS}
        self.cnt = {e: 0 for e in ENGS}
        self.waited = {e: {} for e in ENGS}
        self.lastw = {}
        self.readers = {}
        self.dcnt = {}

    def _deps(self, eng, reads, writes, skip_src=None):
        deps = set()
        for t in reads:
            if t in self.lastw:
                deps.add(self.lastw[t])
        for t in writes:
            if t in self.lastw:
                deps.add(self.lastw[t])
            for src, s in self.readers.get(t, {}).items():
                deps.add((src, s))
        for (src, s) in sorted(deps):
            if src == skip_src:
                continue
            if src == eng:
                if s > self.cnt[eng] or not SAME_ENG_SYNC or eng == "pe":
                    continue
            if self.waited[eng].get(src, 0) >= s:
                continue
            self.q[eng].append(("wait", src, s))
            self.waited[eng][src] = s

    def op(self, eng, fn, reads=(), writes=(), signal=True):
        if self.plan:
            return
        self._deps(eng, reads, writes)
        seq = self.cnt[eng] + 1
        self.q[eng].append(("op", fn, signal))
        if signal:
            self.cnt[eng] = seq
        for t in reads:
            self.readers.setdefault(t, {})[eng] = seq
        for t in writes:
            self.lastw[t] = (eng, seq)
            self.readers[t] = {}

    def dma(self, queue, dsem, out, in_, reads=(), writes=()):
        if self.plan:
            return
        src = "dma:" + dsem
        self._deps(queue, reads, writes, skip_src=src)
        seq = self.dcnt.get(src, 0) + 1
        self.dcnt[src] = seq
        self.q[queue].append(("dma", out, in_, src))
        for t in reads:
            self.readers.setdefault(t, {})[src] = seq
        for t in writes:
            self.lastw[t] = (src, seq)
            self.readers[t] = {}

    def wait_all(self, eng):
        if self.plan:
            return
        for e in ENGS:
            if e != eng and self.cnt[e] > self.waited[eng].get(e, 0):
                self.q[eng].append(("wait", e, self.cnt[e]))
                self.waited[eng][e] = self.cnt[e]
        for src, s in self.dcnt.items():
            if s > self.waited[eng].get(src, 0):
                self.q[eng].append(("wait", src, s))
                self.waited[eng][src] = s

    def barrier(self):
        for e in ENGS:
            self.wait_all(e)

    def emit(self):
        nc = self.nc
        sems = {}
        with contextlib.ExitStack() as st:
            for e in ENGS:
                sems[e] = st.enter_context(nc.semaphore("s_" + e))
            for src in self.dcnt:
                sems[src] = st.enter_context(nc.semaphore("s_" + src.replace(":", "_")))
            block = st.enter_context(nc.Block())
            handles = {"pe": block.tensor, "act": block.scalar, "dve": block.vector,
                       "pool": block.gpsimd, "sp": block.sync}

            def make(e):
                def body(engine):
                    for item in self.q[e]:
                        if item[0] == "wait":
                            _, src, s = item
                            engine.wait_ge(sems[src], s * 16 if src.startswith("dma:") else s)
                        elif item[0] == "op":
                            _, fn, signal = item
                            ins = fn(engine)
                            if signal:
                                ins.then_inc(sems[e], 1)
                        else:
                            _, out, in_, src = item
                            engine.dma_start(out=out, in_=in_).then_inc(sems[src], 16)
                return body

            for e in ENGS:
                handles[e](make(e))


class WStream:
    def __init__(self, P, ring):
        self.P = P
        self.ring = ring
        self.fills = []
        self.i = 0
        self.issued = 0
        self.fences = []
        self.released = 0

    def start_real(self):
        self.i = 0
        self.issued = 0
        self.released = 0

    def fence(self):
        if self.P.plan:
            self.fences.append(len(self.fills))

    def release(self):
        if not self.P.plan:
            self.released += 1

    def _issue(self, n):
        s = n % NRING
        for part in self.fills[n]:
            dst_fn, src = part[0], part[1]
            rd = list(part[2]) if len(part) > 2 else []
            self.P.dma("pool", f"ring{s}", dst_fn(self.ring[s]), src, reads=rd, writes=[("ring", s)])

    def next(self, parts):
        if self.P.plan:
            self.fills.append(parts)
            return self.ring[0], ("ring", 0)
        i = self.i
        self.i += 1
        lim = self.fences[self.released] if self.released < len(self.fences) else len(self.fills)
        while self.issued < min(lim, i + NRING):
            self._issue(self.issued)
            self.issued += 1
        s = i % NRING
        return self.ring[s], ("ring", s)


EV_DSA = True
EV_RET = True
DEBUG = False
DBG_MAP = {}
ALL_STAGES = ("f1_0", "mix_0", "f2_0", "f1_1", "mix_1", "f2_1")


def build(stages=ALL_STAGES):
    nc = bass.Bass("TRN2", target_bir_lowering=False)

    def din(name, shape):
        return nc.dram_tensor(name, list(shape), F32, kind="ExternalInput").ap()

    x = din("x", [S, D])
    meta = din("meta_tokens", [NM, D])
    f1_in = din("ffn1_w_in", [2, D, 2 * DFF])
    f1_out = din("ffn1_w_out", [2, DFF, D])
    f2_in = din("ffn2_w_in", [2, D, 2 * DFF])
    f2_out = din("ffn2_w_out", [2, DFF, D])
    vecs = din("vecs", [128, 64])
    ident_d = din("ident", [128, 128])
    invcnt_d = din("invcnt", [128, 64])
    ev_in = din("ev_w_in", [1, D, 5192])
    ev_out = din("ev_w_out", [1, 1536, D])
    ctab_d = din("ctab", [128, 32])
    cs_d = din("cs_tab", [2, 128, T])
    perm_d = din("perm_tab", [128, 128])
    dt_d = din("dt_tab", [128, 8, 128])
    qdect_d = din("qdect_tab", [128, 4, 128])
    rc_d = din("rc_tab", [128, 64])
    od_in = din("od_w_in", [1, D, 1280])
    od_out = din("od_w_out", [1, D, D])
    od_dmix = din("od_d_mix", [1, 4, 128, 128])
    out = nc.dram_tensor("out", [S, D], F32, kind="ExternalOutput").ap()
    dbg = nc.dram_tensor("dbg", [128, 24576], F32, kind="ExternalOutput").ap() if DEBUG else None
    dbg_pos = [0]
    dbg_map = {}

    def dump(name, ap2d, toks):
        if not DEBUG or P.plan:
            return
        nn = ap2d.shape[1]
        c0_ = dbg_pos[0]
        dbg_pos[0] += nn
        dbg_map[name] = (c0_, nn, ap2d.shape[0])
        P.dma("pool", "dbg", dbg[0:ap2d.shape[0], c0_:c0_ + nn], ap2d, reads=toks)
    global DBG_MAP
    DBG_MAP = dbg_map

    P = Prog(nc)

    hT = nc.alloc_sbuf_tensor("hT", [128, KC, T], F32).ap()
    vec_sb = nc.alloc_sbuf_tensor("vec_sb", [128, 64], F32).ap()
    invcnt = nc.alloc_sbuf_tensor("invcnt_sb", [128, 64], F32).ap()
    esink = nc.alloc_sbuf_tensor("esink", [128, 8], F32).ap()
    ctab = nc.alloc_sbuf_tensor("ctab_sb", [128, 32], F32).ap()
    half_c = nc.alloc_sbuf_tensor("half_c", [128, 1], F32).ap()
    bvec = nc.alloc_sbuf_tensor("bvec", [128, 48], F32).ap()
    den = nc.alloc_sbuf_tensor("den", [128, 8], F32).ap()
    rec = nc.alloc_sbuf_tensor("rec", [128, 8], F32).ap()
    bd_ones = nc.alloc_sbuf_tensor("bd_ones", [128, 128], BF16).ap()
    dmix_sb = nc.alloc_sbuf_tensor("dmix_sb", [128, 4, 128], BF16).ap()
    ident_f = nc.alloc_sbuf_tensor("ident_f", [128, 128], F32).ap()
    ident_b = nc.alloc_sbuf_tensor("ident_b", [128, 128], BF16).ap()
    ones_b = nc.alloc_sbuf_tensor("ones_b", [128, 128], BF16).ap()
    ring = [nc.alloc_sbuf_tensor(f"ring{i}", [128, RING_ELEMS], BF16).ap() for i in range(NRING)]
    xn = nc.alloc_sbuf_tensor("xn", [128, KC, 1040], BF16).ap()
    sq = nc.alloc_sbuf_tensor("sq", [128, KC, 512], BF16).ap()
    lnv = nc.alloc_sbuf_tensor("lnv", [128, 512], F32).ap()
    rstd = nc.alloc_sbuf_tensor("rstd", [128, 512], F32).ap()
    sil = [nc.alloc_sbuf_tensor(f"sil{i}", [128, 512], F32).ap() for i in range(2)]
    ARENA_B = 73000
    arena = nc.alloc_sbuf_tensor("arena", [128, ARENA_B // 2], BF16).ap()

    def carve(off, shape, dt, base=None):
        base = arena if base is None else base
        nel = int(np.prod(shape))
        nb = nel * (4 if dt == F32 else 2)
        assert off % 4 == 0 and off + nb <= base.shape[1] * 2, (off, shape)
        v = base[:, off // 2:(off + nb) // 2]
        if dt == F32:
            v = v.bitcast(F32)
        if len(shape) == 2:
            return v.rearrange("p (a b) -> p a b", a=shape[0])
        if len(shape) == 3:
            return v.rearrange("p (a b c) -> p a b c", a=shape[0], b=shape[1])
        return v

    aT = carve(0, [FC, 1040], BF16)
    io = [carve(45760 + i * 4096, [D], F32) for i in range(2)]
    cqT = carve(0, [4, T], BF16)
    ckT = carve(16512, [T], BF16)
    cv = carve(20640, [17, 2, 65], BF16)
    dxT = carve(25088, [4, T], F32)
    akT = carve(0, [4, T], BF16)
    aqT = carve(16512, [4, T], BF16)
    iqT = carve(33024, [4, T], BF16)
    ikT2 = carve(49536, [T], BF16)
    av = carve(53664, [17, 8, 65], BF16)
    iwabs = carve(71344, [17, 8], F32)
    iwsgn = carve(71888, [17, 8], F32)
    xn_flat = xn.rearrange("p k n -> p (k n)")
    score_b = [carve(0, [T], F32, base=xn_flat), carve(0, [T], F32, base=ring[0])]
    mbq = [carve(8256 + i * 4128, [T], BF16, base=xn_flat) for i in range(2)]
    ptmp = [carve(i * 2112, [528], F32, base=xn_flat) for i in range(2)]
    yg = carve(4224, [4, 512], BF16, base=xn_flat)
    ydT = carve(8320, [4, 512], BF16, base=xn_flat)
    ycT = carve(12416, [4, 512], BF16, base=xn_flat)
    bqraw = carve(0, [4, 512], BF16)
    bkraw = carve(4096, [4, 512], BF16)
    bqr = carve(8192, [4, 512], BF16)
    bkr = carve(12288, [4, 512], BF16)
    bqd = carve(16384, [4, 512], BF16)
    bv_tok = carve(20480, [4, 1024], BF16)
    cs_sb = carve(28672, [2, 512], F32)
    bgs_tok = carve(49536, [4, 1024], BF16)
    ybT_blk = carve(57728, [8, 512], BF16)
    kd_tok = carve(65920, [512], BF16)
    ATb = [carve(66944 + i * 256, [128], BF16) for i in range(2)]
    S_sb = carve(67456, [4, 128], F32)
    S_bf = carve(69504, [4, 128], BF16)
    yb_tok = carve(70528, [1024], BF16)
    xnR = carve(0, [8, 512], BF16, base=xn_flat)
    DT = carve(8192, [8, 128], F32, base=xn_flat)
    qdecT = carve(12288, [4, 128], F32, base=xn_flat)
    perm_sb = carve(14336, [128], BF16, base=xn_flat)
    rc = carve(14592, [64], F32, base=xn_flat)
    sq_flat = sq.rearrange("p k n -> p (k n)")
    otmp = carve(0, [1024], BF16, base=sq_flat)
    junk = carve(0, [T], BF16, base=sq_flat)
    Eb = [carve(4352 + i * 1024, [512], BF16, base=sq_flat) for i in range(2)]
    ident30k = carve(0, [128], BF16, base=lnv.bitcast(BF16))
    Dh = carve(0, [8, 128], BF16, base=rstd.bitcast(BF16))
    relub = [carve(j_ * 1024, [512], BF16, base=sil[i_].bitcast(BF16)) for i_ in range(2) for j_ in range(2)]
    yatok = carve(1024, [512], BF16, base=lnv.bitcast(BF16))
    Et = [carve(i * 768, [384], BF16, base=sq_flat) for i in range(2)]
    yctok = carve(1536, [512], BF16, base=sq_flat)
    sqh = carve(2560, [512], BF16, base=sq_flat)
    sqh2 = [sqh, carve(3584, [512], BF16, base=sq_flat)]
    ps = [nc.alloc_psum_tensor(f"ps{i}", [128, 512], F32).ap() for i in range(8)]

    W = WStream(P, ring)

    def program():
        P.dma("sp", "cst0", vec_sb, vecs, writes=["vec"])
        P.dma("sp", "cst1", ident_f, ident_d, writes=["identf"])
        P.op("dve", lambda e: e.tensor_copy(out=ident_b, in_=ident_f), reads=["identf"], writes=["identb"])
        P.op("dve", lambda e: e.memset(ones_b, 1.0), writes=["ones"])
        P.dma("sp", "cst2", invcnt, invcnt_d, writes=["invcnt"])
        P.dma("sp", "cst4", ctab, ctab_d, writes=["ctab"])
        P.dma("pool", "cst3", dmix_sb, od_dmix[0].rearrange("g c e -> c g e"), writes=["dmix"])
        P.op("pool", lambda e: e.memset(bd_ones, 0.0), writes=["bd"], signal=False)
        P.op("pool", lambda e: e.memset(bd_ones[0:64, 0:64], 1.0), writes=["bd"], signal=False)
        P.op("pool", lambda e: e.memset(bd_ones[64:128, 64:128], 1.0), writes=["bd"])

        def load_tile(ti, src_ap, nrow, col0):
            buf = io[ti % 2]
            P.dma("sp", f"io{ti%2}", buf[0:nrow, :], src_ap, writes=[("io", ti % 2)])
            for half in range(2):
                pt = ps[6 + half]
                for kk in range(4):
                    k = half * 4 + kk
                    P.op("pe", lambda e, k=k, kk=kk, pt=pt, buf=buf: e.transpose(
                        out=pt[:, kk * 128: kk * 128 + nrow], in_=buf[0:nrow, k * 128:(k + 1) * 128],
                        identity=ident_f[0:nrow, 0:nrow]),
                        reads=[("io", ti % 2), "identf"], writes=[("ps", 6 + half)], signal=(kk == 3))
                P.op("dve" if half == 0 else "act",
                     (lambda e, pt=pt, half=half: e.tensor_copy(
                         out=hT[:, half * 4:(half + 1) * 4, col0:col0 + nrow],
                         in_=pt.rearrange("p (k n) -> p k n", k=4)[:, :, 0:nrow])) if half == 0 else
                     (lambda e, pt=pt, half=half: e.copy(
                         out=hT[:, half * 4:(half + 1) * 4, col0:col0 + nrow],
                         in_=pt.rearrange("p (k n) -> p k n", k=4)[:, :, 0:nrow])),
                     reads=[("ps", 6 + half)], writes=[("hld", ti, half)])

        load_tile(0, meta, NM, 0)
        for m in range(16):
            load_tile(m + 1, x[m * 128:(m + 1) * 128, :], 128, NM + m * 128)
        P.barrier()

        def norm_block(b, gcol, xoff, dst=None, dtok=None, rng=None, hname="h"):
            dst = xn if dst is None else dst
            dtok = ("xn", xoff) if dtok is None else dtok
            c0, c1 = BLKS[b] if rng is None else rng
            n = c1 - c0
            hk = [(hname, b, k) for k in range(KC)]
            P.op("act", lambda e: e.activation(out=sq[:, :, 0:n], in_=hT[:, :, c0:c1], func=AF.Square),
                 reads=hk, writes=["sq"])
            for k in range(KC):
                P.op("pe", lambda e, k=k: e.matmul(ps[6][:, 0:n], lhsT=ones_b, rhs=sq[:, k, 0:n],
                                                   start=(k == 0), stop=(k == KC - 1)),
                     reads=["sq", "ones"], writes=[("ps", 6)], signal=(k == KC - 1))
            P.op("act", lambda e: e.activation(out=lnv[:, 0:n], in_=ps[6][:, 0:n], func=AF.Ln,
                                               scale=1.0 / D, bias=eps_sb),
                 reads=[("ps", 6), "eps"], writes=["lnv"])
            P.op("act", lambda e: e.activation(out=rstd[:, 0:n], in_=lnv[:, 0:n], func=AF.Exp, scale=-0.5),
                 reads=["lnv"], writes=["rstd"])
            for k in range(KC):
                P.op("dve", lambda e, k=k: e.scalar_tensor_tensor(
                    out=dst[:, k, xoff:xoff + n], in0=hT[:, k, c0:c1], scalar=vec_sb[:, gcol + k:gcol + k + 1],
                    op0=ALU.mult, in1=rstd[:, 0:n], op1=ALU.mult),
                    reads=[(hname, b, k), "rstd", "vec"], writes=[dtok], signal=(k == KC - 1))

        def ffn(w_in, w_out, gcol):
            win_v = w_in.rearrange("(k p) f -> p k f", p=128)
            wout_v = w_out.rearrange("(j p) d -> p j d", p=128)
            FB = [(i * 344, (i + 1) * 344) for i in range(6)]
            BLKS = FB
            for half in ([0, 1, 2], [3, 4, 5]):
                xoffs = {}
                for b in half:
                    xoffs[b] = (b % 3) * 344
                    norm_block(b, gcol, xoffs[b], rng=FB[b], hname="hf")
                cnt = 0
                for fg in range(FC // 2):
                    def dg(slot):
                        return slot[:, 0:4096].rearrange("p (k t c) -> p k t c", k=8, t=2, c=256)[:, :, 0, :]

                    def du(slot):
                        return slot[:, 0:4096].rearrange("p (k t c) -> p k t c", k=8, t=2, c=256)[:, :, 1, :]
                    slot, tok = W.next([(dg, win_v[:, :, fg * 256:(fg + 1) * 256]),
                                        (du, win_v[:, :, DFF + fg * 256:DFF + (fg + 1) * 256])])
                    for b in half:
                        n = BLKS[b][1] - BLKS[b][0]
                        xo = xoffs[b]
                        for j in range(2):
                            pg = cnt % 2
                            pu = 2 + cnt % 2
                            cnt += 1
                            for k in range(KC):
                                P.op("pe", lambda e, k=k, j=j, slot=slot, pg=pg, xo=xo, n=n: e.matmul(
                                    ps[pg][:, 0:n], lhsT=slot[:, k * 512 + j * 128:k * 512 + j * 128 + 128],
                                    rhs=xn[:, k, xo:xo + n], start=(k == 0), stop=(k == KC - 1)),
                                    reads=[tok, ("xn", xo)], writes=[("ps", pg)], signal=(k == KC - 1))
                            for k in range(KC):
                                P.op("pe", lambda e, k=k, j=j, slot=slot, pu=pu, xo=xo, n=n: e.matmul(
                                    ps[pu][:, 0:n], lhsT=slot[:, k * 512 + 256 + j * 128:k * 512 + 256 + j * 128 + 128],
                                    rhs=xn[:, k, xo:xo + n], start=(k == 0), stop=(k == KC - 1)),
                                    reads=[tok, ("xn", xo)], writes=[("ps", pu)], signal=(k == KC - 1))
                            sb = sil[pg]
                            P.op("act", lambda e, pg=pg, n=n, sb=sb: e.activation(out=sb[:, 0:n], in_=ps[pg][:, 0:n], func=AF.Silu),
                                 reads=[("ps", pg)], writes=[("sil", pg)])
                            fch = fg * 2 + j
                            P.op("dve", lambda e, sb=sb, pu=pu, n=n, fch=fch, xo=xo: e.tensor_tensor(
                                out=aT[:, fch, xo:xo + n], in0=sb[:, 0:n], in1=ps[pu][:, 0:n], op=ALU.mult),
                                reads=[("sil", pg), ("ps", pu)], writes=[("aT", fch, xo)])
                cnt = 0
                for dgi in range(4):
                    def do(slot):
                        return slot[:, 0:FC * 256].rearrange("p (j c) -> p j c", j=FC, c=256)
                    slot, tok = W.next([(do, wout_v[:, :, dgi * 256:(dgi + 1) * 256])])
                    for b in half:
                        n = BLKS[b][1] - BLKS[b][0]
                        xo = xoffs[b]
                        c0, c1 = BLKS[b]
                        for c in range(2):
                            py = 4 + cnt % 2
                            cnt += 1
                            dch = dgi * 2 + c
                            for j in range(FC):
                                P.op("pe", lambda e, j=j, c=c, slot=slot, py=py, xo=xo, n=n: e.matmul(
                                    ps[py][:, 0:n], lhsT=slot[:, j * 256 + c * 128:j * 256 + c * 128 + 128],
                                    rhs=aT[:, j, xo:xo + n], start=(j == 0), stop=(j == FC - 1)),
                                    reads=[tok, ("aT", j, xo)], writes=[("ps", py)], signal=(j == FC - 1))
                            P.op("dve", lambda e, py=py, n=n, dch=dch, c0=c0, c1=c1: e.scalar_tensor_tensor(
                                out=hT[:, dch, c0:c1], in0=ps[py][:, 0:n], scalar=0.5, op0=ALU.mult,
                                in1=hT[:, dch, c0:c1], op1=ALU.add),
                                reads=[("ps", py), ("hf", b, dch)], writes=[("hf", b, dch)])


        XO = {0: 0, 1: 16, 2: 528, 3: 16, 4: 528}

        def pipeline(items):
            if not items:
                return
            items[0][0]()
            for i_ in range(len(items)):
                if i_ + 1 < len(items):
                    items[i_ + 1][0]()
                items[i_][1]()

        ps7b = ps[7].bitcast(BF16)

        hn_cnt = [0]

        def hn_front(psrc, n, src_tok, i):
            P.op("act", lambda e: e.activation(out=sqh2[i][:, 0:n], in_=psrc[:, 0:n], func=AF.Square),
                 reads=[src_tok], writes=[("sqh", i)])

        def hn_back(psrc, n, gcol, dst, dst_tok, src_tok, i):
            P.op("pe", lambda e: e.matmul(ps[6][:, 0:n], lhsT=bd_ones, rhs=sqh2[i][:, 0:n], start=True, stop=True),
                 reads=[("sqh", i), "bd"], writes=[("ps", 6)])
            P.op("act", lambda e: e.activation(out=lnv[:, 0:n], in_=ps[6][:, 0:n], func=AF.Ln,
                                               scale=1.0 / 64, bias=eps_sb),
                 reads=[("ps", 6), "eps"], writes=["lnv"])
            P.op("act", lambda e: e.activation(out=rstd[:, 0:n], in_=lnv[:, 0:n], func=AF.Exp, scale=-0.5),
                 reads=["lnv"], writes=["rstd"])
            P.op("dve", lambda e: e.scalar_tensor_tensor(
                out=dst, in0=psrc[:, 0:n], scalar=vec_sb[:, gcol:gcol + 1], op0=ALU.mult,
                in1=rstd[:, 0:n], op1=ALU.mult),
                reads=[src_tok, "rstd", "vec"], writes=[dst_tok])

        def headnorm(psrc, n, gcol, dst, dst_tok, src_tok):
            i = hn_cnt[0] % 2
            hn_cnt[0] += 1
            hn_front(psrc, n, src_tok, i)
            hn_back(psrc, n, gcol, dst, dst_tok, src_tok, i)

        def proj_headnorm_items(slot, tok, blocks, nextps_fn, gcol, dstT, which, ncw=512):
            items = []
            for b in blocks:
                c0, c1 = BLKS[b]
                n = c1 - c0
                xo = XO[b]
                for c in range(4):
                    i = hn_cnt[0] % 2
                    hn_cnt[0] += 1
                    st_ = {}

                    def front(c=c, n=n, xo=xo, i=i, st_=st_):
                        pq = nextps_fn()
                        st_["pq"] = pq
                        for k in range(KC):
                            P.op("pe", lambda e, k=k, c=c, pq=pq, xo=xo, n=n: e.matmul(
                                ps[pq][:, 0:n], lhsT=slot[:, k * ncw + c * 128:k * ncw + c * 128 + 128],
                                rhs=xn[:, k, xo:xo + n], start=(k == 0), stop=(k == KC - 1)),
                                reads=[tok, ("xn", xo)], writes=[("ps", pq)], signal=(k == KC - 1))
                        hn_front(ps[pq], n, ("ps", pq), i)

                    def back(c=c, n=n, i=i, st_=st_, c0=c0, c1=c1, b=b):
                        pq = st_["pq"]
                        hn_back(ps[pq], n, gcol, dstT[:, c, c0:c1], (which, c, b), ("ps", pq), i)
                    items.append((front, back))
            return items

        def odd_mixer():
            win_v = od_in[0].rearrange("(k p) f -> p k f", p=128)
            P.op("pool", lambda e: e.memset(cv[:, :, :, 64:65], 1.0), writes=["cv1"])
            P.op("act", lambda e: e.activation(out=esink, in_=vec_sb[:, 56:64], func=AF.Exp),
                 reads=["vec"], writes=["esink"])
            pcnt = [0]

            def nextps():
                pcnt[0] += 1
                return pcnt[0] % 4

            for half in HALVES:
                for b in half:
                    norm_block(b, 16 + 8, XO[b])
                slot, tok = W.next([(lambda slot: slot[:, 0:4096].rearrange("p (k f) -> p k f", k=8), win_v[:, :, 0:512])])
                pipeline(proj_headnorm_items(slot, tok, half, nextps, 48, cqT, "cq"))
                slot, tok = W.next([(lambda slot: slot[:, 0:2048].rearrange("p (k f) -> p k f", k=8), win_v[:, :, 512:768])])
                for b in half:
                    c0, c1 = BLKS[b]
                    n = c1 - c0
                    xo = XO[b]
                    pq = nextps()
                    for k in range(KC):
                        P.op("pe", lambda e, k=k, slot=slot, pq=pq, xo=xo, n=n: e.matmul(
                            ps[pq][:, 0:n], lhsT=slot[:, k * 256:k * 256 + 128],
                            rhs=xn[:, k, xo:xo + n], start=(k == 0), stop=(k == KC - 1)),
                            reads=[tok, ("xn", xo)], writes=[("ps", pq)], signal=(k == KC - 1))
                    headnorm(ps[pq], n, 49, ckT[:, c0:c1], ("ck", b), ("ps", pq))
                    ntile = 1 if b == 0 else 4
                    for j in range(ntile):
                        nt = 16 if b == 0 else 128
                        ti = 0 if b == 0 else 1 + (b - 1) * 4 + j
                        pq = nextps()
                        for k in range(KC):
                            P.op("pe", lambda e, k=k, slot=slot, pq=pq, xo=xo, j=j, nt=nt: e.matmul(
                                ps[pq][0:nt, 0:128], lhsT=xn[:, k, xo + j * 128:xo + j * 128 + nt],
                                rhs=slot[:, k * 256 + 128:k * 256 + 256], start=(k == 0), stop=(k == KC - 1)),
                                reads=[tok, ("xn", xo)], writes=[("ps", pq)], signal=(k == KC - 1))
                        P.op("act", lambda e, pq=pq, nt=nt, ti=ti: e.copy(
                            out=cv[0:nt, ti, :, 0:64], in_=ps[pq][0:nt, 0:128].rearrange("p (g d) -> p g d", g=2)),
                            reads=[("ps", pq)], writes=[("cv", ti)])
                slot, tok = W.next([(lambda slot: slot[:, 0:4096].rearrange("p (k f) -> p k f", k=8), win_v[:, :, 768:1280])])
                for b in half:
                    c0, c1 = BLKS[b]
                    n = c1 - c0
                    xo = XO[b]
                    for g in range(4):
                        pq = nextps()
                        for k in range(KC):
                            P.op("pe", lambda e, k=k, g=g, slot=slot, pq=pq, xo=xo, n=n: e.matmul(
                                ps[pq][:, 0:n], lhsT=slot[:, k * 512 + g * 128:k * 512 + g * 128 + 128],
                                rhs=xn[:, k, xo:xo + n], start=(k == 0), stop=(k == KC - 1)),
                                reads=[tok, ("xn", xo)], writes=[("ps", pq)], signal=(k == KC - 1))
                        P.op("act", lambda e, pq=pq, g=g, c0=c0, c1=c1, n=n: e.copy(out=dxT[:, g, c0:c1], in_=ps[pq][:, 0:n]),
                             reads=[("ps", pq)], writes=[("dx", g, b)])
            P.barrier()

            ecnt = [0]
            for b in range(5):
                c0, c1 = BLKS[b]
                n = c1 - c0
                tiles = [(-1, 16, 0)] if b == 0 else [(4 * (b - 1) + j, 128, 16 + 128 * (4 * (b - 1) + j)) for j in range(4)]
                for (m, nq, q0) in tiles:
                    qb = 0 if m < 0 else (m - 4 * (b - 1)) * 128
                    items = []
                    for h in range(8):
                        c = h % 4
                        hh = h // 4
                        pb = hh * 64
                        ei = ecnt[0] % 2
                        ecnt[0] += 1
                        E = Et[ei]
                        rq = cqT[pb:pb + 64, c, q0:q0 + nq]
                        segs = [(0, 16, 0, 0)]
                        if m >= 1:
                            segs.append((1, 128, 16 + 128 * (m - 1), m))
                        if m >= 0:
                            segs.append((2, 128, 16 + 128 * m, 1 + m))

                        def front(c=c, pb=pb, ei=ei, E=E, rq=rq, segs=segs, m=m, nq=nq):
                            pq = nextps()
                            rd = [("cq", c, bb) for bb in range(5)] + [("ck", bb) for bb in range(5)]
                            for si, (sg, nk, k0, _) in enumerate(segs):
                                P.op("pe", lambda e, sg=sg, nk=nk, k0=k0, pb=pb, pq=pq, rq=rq, nq=nq: e.matmul(
                                    ps[pq][0:nk, sg * 128:sg * 128 + nq], lhsT=ckT[pb:pb + 64, k0:k0 + nk], rhs=rq,
                                    start=True, stop=True),
                                    reads=rd, writes=[("ps", pq)], signal=(si == len(segs) - 1))
                            P.op("act", lambda e, E=E, pq=pq, nq=nq: e.activation(
                                out=E[0:16, 0:nq], in_=ps[pq][0:16, 0:nq], func=AF.Exp, scale=0.125),
                                reads=[("ps", pq)], writes=[("E", ei)], signal=(m < 0))
                            if m >= 0:
                                lo = 128 if m >= 1 else 256
                                P.op("act", lambda e, E=E, pq=pq, lo=lo: e.activation(
                                    out=E[:, lo:384], in_=ps[pq][:, lo:384], func=AF.Exp, scale=0.125),
                                    reads=[("ps", pq)], writes=[("E", ei)])
                                if m >= 1:
                                    P.op("pool", lambda e, E=E: e.memset(E[0:64, 192:256], 0.0),
                                         reads=[], writes=[("E", ei)], signal=False)
                                P.op("pool", lambda e, E=E: e.memset(E[64:128, 256:320], 0.0),
                                     reads=[], writes=[("E", ei)])

                        def back(c=c, hh=hh, ei=ei, E=E, segs=segs, nq=nq):
                            po = 4 + hh
                            for si, (sg, nk, k0, ti) in enumerate(segs):
                                P.op("pe", lambda e, sg=sg, nk=nk, ti=ti, E=E, po=po, c=c, hh=hh, nq=nq, si=si, ns=len(segs): e.matmul(
                                    ps[po][0:nq, c * 65:c * 65 + 65], lhsT=E[0:nk, sg * 128:sg * 128 + nq],
                                    rhs=cv[0:nk, ti, hh, :], start=(si == 0), stop=(si == ns - 1)),
                                    reads=[("E", ei), ("cv", ti), "cv1"], writes=[("ps", po)], signal=(si == len(segs) - 1))
                        items.append((front, back))
                    pipeline(items)
                    for hh in range(2):
                        po = 4 + hh
                        pv = ps[po][0:nq, 0:260].rearrange("p (c e) -> p c e", e=65)
                        P.op("dve", lambda e, pv=pv, hh=hh, nq=nq: e.tensor_tensor(
                            out=den[0:nq, hh * 4:hh * 4 + 4], in0=pv[:, :, 64], in1=esink[0:nq, hh * 4:hh * 4 + 4], op=ALU.add),
                            reads=[("ps", po), "esink"], writes=[("den", hh)])
                        P.op("dve", lambda e, hh=hh, nq=nq: e.reciprocal(out=rec[0:nq, hh * 4:hh * 4 + 4], in_=den[0:nq, hh * 4:hh * 4 + 4]),
                             reads=[("den", hh)], writes=[("rec", hh)])
                        P.op("dve", lambda e, pv=pv, hh=hh, nq=nq: e.tensor_tensor(
                            out=yctok[0:nq, :].rearrange("p (c x d) -> p c x d", c=4, x=2)[:, :, hh, :],
                            in0=pv[:, :, 0:64], in1=rec[0:nq, hh * 4:hh * 4 + 4].unsqueeze(2).to_broadcast([nq, 4, 64]),
                            op=ALU.mult),
                            reads=[("ps", po), ("rec", hh)], writes=[("yctok", hh)])
                    for c in range(4):
                        P.op("pe", lambda e, c=c, nq=nq: e.transpose(
                            out=ps7b[:, c * 128:c * 128 + nq], in_=yctok[0:nq, c * 128:(c + 1) * 128],
                            identity=ident_b[0:nq, 0:nq]),
                            reads=[("yctok", 0), ("yctok", 1), "identb"], writes=[("ps", 7)], signal=(c == 3))
                    P.op("act", lambda e, nq=nq, qb=qb: e.copy(
                        out=ycT[:, :, qb:qb + nq], in_=ps7b[:, 0:512].rearrange("p (c q) -> p c q", c=4)[:, :, 0:nq]),
                        reads=[("ps", 7)], writes=[("ycT", qb)])
                LB = 0 if b == 0 else 16
                Wd = n + LB
                for g, w in enumerate((2, 4, 8, 16)):
                    src = dxT[:, g, c0 - LB:c1]
                    cur = src
                    cur_tok = [("dx", g, bb) for bb in range(5)]
                    for lv in range(g + 1):
                        sh = 1 << lv
                        dst = ptmp[lv % 2]
                        P.op("pool", lambda e, dst=dst, cur=cur, sh=sh: e.tensor_copy(out=dst[:, 0:sh], in_=cur[:, 0:sh]),
                             reads=cur_tok, writes=[("ptmp", lv % 2)], signal=False)
                        P.op("pool", lambda e, dst=dst, cur=cur, sh=sh, Wd=Wd: e.tensor_tensor(
                            out=dst[:, sh:Wd], in0=cur[:, sh:Wd], in1=cur[:, 0:Wd - sh], op=ALU.add),
                            reads=cur_tok, writes=[("ptmp", lv % 2)])
                        cur = dst
                        cur_tok = [("ptmp", lv % 2)]
                    if b == 0:
                        P.op("dve", lambda e, cur=cur, g=g: e.tensor_tensor(
                            out=lnv[:, 0:16], in0=cur[:, 0:16], in1=invcnt[:, g * 16:(g + 1) * 16], op=ALU.mult),
                            reads=cur_tok + ["invcnt"], writes=["lnv"])
                        P.op("dve", lambda e, g=g: e.tensor_tensor(
                            out=yg[:, g, 0:16], in0=lnv[:, 0:16], in1=dxT[:, g, 0:16], op=ALU.subtract),
                            reads=["lnv", ("dx", g, 0)], writes=[("yg", g)])
                    else:
                        P.op("dve", lambda e, cur=cur, g=g, w=w, n=n, c0=c0, c1=c1: e.scalar_tensor_tensor(
                            out=yg[:, g, 0:n], in0=cur[:, 16:16 + n], scalar=1.0 / w, op0=ALU.mult,
                            in1=dxT[:, g, c0:c1], op1=ALU.subtract),
                            reads=cur_tok + [("dx", g, b)], writes=[("yg", g)])
                    pq = nextps()
                    P.op("pe", lambda e, g=g, pq=pq, n=n: e.matmul(ps[pq][:, 0:n], lhsT=dmix_sb[:, g, :], rhs=yg[:, g, 0:n],
                                                                 start=True, stop=True),
                         reads=[("yg", g), "dmix"], writes=[("ps", pq)])
                    P.op("dve", lambda e, g=g, pq=pq, n=n: e.tensor_scalar(
                        out=ydT[:, g, 0:n], in0=ps[pq][:, 0:n], scalar1=vec_sb[:, 52 + g:53 + g], scalar2=None, op0=ALU.mult),
                        reads=[("ps", pq), "vec"], writes=[("ydT", g)])
                wo = od_out[0]
                for dh in range(2):
                    slot, tok = W.next([(lambda slot: slot[:, 0:4096].rearrange("p (k f) -> p k f", k=8),
                                         wo.rearrange("(j p) d -> p j d", p=128)[:, :, dh * 512:(dh + 1) * 512])])
                    for dc in range(4):
                        pq = nextps()
                        dch = dh * 4 + dc
                        for j in range(8):
                            rhs = ycT[:, j, 0:n] if j < 4 else ydT[:, j - 4, 0:n]
                            rtok = [("ycT", qq) for qq in (0, 128, 256, 384)] if j < 4 else [("ydT", j - 4)]
                            P.op("pe", lambda e, j=j, dc=dc, slot=slot, pq=pq, rhs=rhs, n=n: e.matmul(
                                ps[pq][:, 0:n], lhsT=slot[:, j * 512 + dc * 128:j * 512 + dc * 128 + 128], rhs=rhs,
                                start=(j == 0), stop=(j == 7)),
                                reads=[tok] + rtok, writes=[("ps", pq)], signal=(j == 7))
                        P.op("dve", lambda e, pq=pq, n=n, dch=dch, c0=c0, c1=c1: e.tensor_tensor(
                            out=hT[:, dch, c0:c1], in0=ps[pq][:, 0:n], in1=hT[:, dch, c0:c1], op=ALU.add),
                            reads=[("ps", pq), ("h", b, dch)], writes=[("h", b, dch)])
            P.barrier()


        IWS = (8 ** -0.5) * (64 ** -0.5)
        NIT = 20

        def even_mixer(do_ret=True):
            win_v = ev_in[0].rearrange("(k p) f -> p k f", p=128)
            P.op("pool", lambda e: e.memset(av[:, :, :, 64:65], 1.0), writes=["av1"])
            P.op("pool", lambda e: e.memset(half_c, 0.5), writes=["halfc"])
            pcnt = [0]

            def nextps():
                pcnt[0] += 1
                return pcnt[0] % 4

            def full_slot(slot):
                return slot[:, 0:4096].rearrange("p (k f) -> p k f", k=8)

            for half in HALVES:
                for b in half:
                    norm_block(b, 16 + 0, XO[b])
                for which, col0, gcol, dstT in (("aq", 0, 50, aqT), ("ak", 512, 51, akT), ("iq", 1536, None, iqT)):
                    slot, tok = W.next([(full_slot, win_v[:, :, col0:col0 + 512])])
                    if gcol is not None:
                        pipeline(proj_headnorm_items(slot, tok, half, nextps, gcol, dstT, which))
                        continue
                    for b in half:
                        c0, c1 = BLKS[b]
                        n = c1 - c0
                        xo = XO[b]
                        for c in range(4):
                            pq = nextps()
                            for k in range(KC):
                                P.op("pe", lambda e, k=k, c=c, slot=slot, pq=pq, xo=xo, n=n: e.matmul(
                                    ps[pq][:, 0:n], lhsT=slot[:, k * 512 + c * 128:k * 512 + c * 128 + 128],
                                    rhs=xn[:, k, xo:xo + n], start=(k == 0), stop=(k == KC - 1)),
                                    reads=[tok, ("xn", xo)], writes=[("ps", pq)], signal=(k == KC - 1))
                            if True:
                                P.op("act", lambda e, pq=pq, c=c, c0=c0, c1=c1, n=n, dstT=dstT: e.copy(
                                    out=dstT[:, c, c0:c1], in_=ps[pq][:, 0:n]),
                                    reads=[("ps", pq)], writes=[(which, c, b)])
                slot, tok = W.next([(full_slot, win_v[:, :, 1024:1536])])
                for b in half:
                    xo = XO[b]
                    ntile = 1 if b == 0 else 4
                    for j in range(ntile):
                        nt = 16 if b == 0 else 128
                        ti = 0 if b == 0 else 1 + (b - 1) * 4 + j
                        pq = nextps()
                        for k in range(KC):
                            P.op("pe", lambda e, k=k, slot=slot, pq=pq, xo=xo, j=j, nt=nt: e.matmul(
                                ps[pq][0:nt, 0:512], lhsT=xn[:, k, xo + j * 128:xo + j * 128 + nt],
                                rhs=slot[:, k * 512:(k + 1) * 512], start=(k == 0), stop=(k == KC - 1)),
                                reads=[tok, ("xn", xo)], writes=[("ps", pq)], signal=(k == KC - 1))
                        P.op("act", lambda e, pq=pq, nt=nt, ti=ti: e.copy(
                            out=av[0:nt, ti, :, 0:64], in_=ps[pq][0:nt, 0:512].rearrange("p (h d) -> p h d", h=8)),
                            reads=[("ps", pq)], writes=[("av", ti)])
                def sl256(slot):
                    return slot[:, 0:2048].rearrange("p (k f) -> p k f", k=8)
                slot, tok = W.next([
                    (lambda slot: sl256(slot)[:, :, 0:64], win_v[:, :, 2048:2112]),
                    (lambda slot: sl256(slot)[:, :, 64:128], win_v[:, :, 2048:2112]),
                    (lambda slot: sl256(slot)[:, :, 128:136], win_v[:, :, 2112:2120])])
                for b in half:
                    c0, c1 = BLKS[b]
                    n = c1 - c0
                    xo = XO[b]
                    pq = nextps()
                    for k in range(KC):
                        P.op("pe", lambda e, k=k, slot=slot, pq=pq, xo=xo, n=n: e.matmul(
                            ps[pq][:, 0:n], lhsT=slot[:, k * 256:k * 256 + 128],
                            rhs=xn[:, k, xo:xo + n], start=(k == 0), stop=(k == KC - 1)),
                            reads=[tok, ("xn", xo)], writes=[("ps", pq)], signal=(k == KC - 1))
                    P.op("act", lambda e, pq=pq, c0=c0, c1=c1, n=n: e.copy(out=ikT2[:, c0:c1], in_=ps[pq][:, 0:n]),
                         reads=[("ps", pq)], writes=[("ik", b)])
                    ntile = 1 if b == 0 else 4
                    for j in range(ntile):
                        nt = 16 if b == 0 else 128
                        ti = 0 if b == 0 else 1 + (b - 1) * 4 + j
                        pq = nextps()
                        for k in range(KC):
                            P.op("pe", lambda e, k=k, slot=slot, pq=pq, xo=xo, j=j, nt=nt: e.matmul(
                                ps[pq][0:nt, 0:8], lhsT=xn[:, k, xo + j * 128:xo + j * 128 + nt],
                                rhs=slot[:, k * 256 + 128:k * 256 + 136], start=(k == 0), stop=(k == KC - 1)),
                                reads=[tok, ("xn", xo)], writes=[("ps", pq)], signal=(k == KC - 1))
                        P.op("act", lambda e, pq=pq, nt=nt, ti=ti: e.activation(
                            out=iwabs[0:nt, ti, :], in_=ps[pq][0:nt, 0:8], func=AF.Abs, scale=IWS),
                            reads=[("ps", pq)], writes=[("iwabs", ti)])
                        P.op("act", lambda e, pq=pq, nt=nt, ti=ti: e.activation(
                            out=iwsgn[0:nt, ti, :], in_=ps[pq][0:nt, 0:8], func=AF.Sign),
                            reads=[("ps", pq)], writes=[("iwsgn", ti)])
            W.fence()
            P.barrier()

            ALLQ = [(w, c, bb) for w in ("aq", "ak", "iq") for c in range(4) for bb in range(5)] + [("ik", bb) for bb in range(5)]
            ecnt = [0]
            rcnt = [0]
            lo0 = bvec[:, 0:1]
            hi0 = bvec[:, 1:2]
            w0 = bvec[:, 2:3]
            mid = bvec[:, 3:4]
            cnt = bvec[:, 4:5]
            tt = bvec[:, 5:6]
            thr = bvec[:, 6:7]
            steps = bvec[:, 8:8 + NIT + 1]
            P.op("dve", lambda e: e.tensor_scalar(out=ident30k, in0=ident_f, scalar1=30000.0, scalar2=None, op0=ALU.mult),
                 reads=["identf", "lnv"], writes=["i30k", "lnv"])

            def tile_params(m):
                nq = 16 if m < 0 else 128
                q0 = 0 if m < 0 else 16 + 128 * m
                ti = 0 if m < 0 else 1 + m
                KR = 16 if m < 0 else 16 + 128 * (m + 1)
                return nq, q0, ti, KR

            def stageI(m):
                nq, q0, ti, KR = tile_params(m)
                score = score_b[ti % 2]
                sb_ = ti % 2
                kblocks = [(k0, min(k0 + 512, KR)) for k0 in range(0, KR, 512)]
                P.op("dve", lambda e: e.tensor_tensor(
                    out=Dh[0:nq, :, 0:nq], in0=ident_b[0:nq, 0:nq].unsqueeze(1).to_broadcast([nq, 8, nq]),
                    in1=iwsgn[0:nq, ti, :].unsqueeze(2).to_broadcast([nq, 8, nq]), op=ALU.mult),
                    reads=["identb", ("iwsgn", ti)], writes=["Dh"])
                for (k0, k1) in kblocks:
                    kw = k1 - k0
                    pend = []

                    def emit_logits(h, k0=k0, k1=k1, kw=kw, pend=pend):
                        pb = (h % 2) * 64
                        c = h // 2
                        pq = nextps()
                        P.op("pe", lambda e, pb=pb, c=c, pq=pq: e.matmul(
                            ps[pq][0:nq, 0:kw], lhsT=iqT[pb:pb + 64, c, q0:q0 + nq], rhs=ikT2[pb:pb + 64, k0:k1],
                            start=True, stop=True),
                            reads=ALLQ + [("iqya", ti)], writes=[("ps", pq)])
                        ri = rcnt[0] % 4
                        rcnt[0] += 1
                        rt = relub[ri]
                        P.op("act", lambda e, rt=rt, pq=pq, h=h: e.activation(
                            out=rt[0:nq, 0:kw], in_=ps[pq][0:nq, 0:kw], func=AF.Relu, scale=iwabs[0:nq, ti, h:h + 1]),
                            reads=[("ps", pq), ("iwabs", ti)], writes=[("relub", ri)])
                        pend.append((h, rt, ri))

                    def emit_diag(kw=kw, pend=pend):
                        h, rt, ri = pend.pop(0)
                        P.op("pe", lambda e, rt=rt, h=h: e.matmul(
                            ps[6][0:nq, 0:kw], lhsT=Dh[0:nq, h, 0:nq], rhs=rt[0:nq, 0:kw], start=(h == 0), stop=(h == 7)),
                            reads=[("relub", ri), "Dh"], writes=[("ps", 6)], signal=True)

                    emit_logits(0)
                    emit_logits(1)
                    for h in range(2, 8):
                        emit_logits(h)
                        emit_diag()
                    emit_diag()
                    emit_diag()
                    P.op("act", lambda e, kw=kw, k0=k0, k1=k1: e.copy(out=score[0:nq, k0:k1], in_=ps[6][0:nq, 0:kw]),
                         reads=[("ps", 6)], writes=[("score", sb_, k0)])

            def stageS(m):
                nq, q0, ti, KR = tile_params(m)
                score = score_b[ti % 2]
                sb_ = ti % 2
                mb = mbq[ti % 2]
                mbtok = ("mbq", ti % 2)
                kblocks = [(k0, min(k0 + 512, KR)) for k0 in range(0, KR, 512)]
                sc_tok = [("score", sb_, k0) for (k0, _) in kblocks]
                P.op("dve", lambda e: e.tensor_reduce(out=lo0[0:nq], in_=score[0:nq, 0:KR], op=ALU.min, axis=AX.X),
                     reads=sc_tok, writes=["lo0"])
                if KR > 256:
                    P.op("dve", lambda e: e.tensor_reduce(out=hi0[0:nq], in_=score[0:nq, 0:KR], op=ALU.max, axis=AX.X),
                         reads=sc_tok, writes=["hi0"])
                if m >= 0:
                    P.op("dve", lambda e: e.memset(score[0:64, KR - 64:KR], -1e30), reads=["hi0", "lo0"],
                         writes=sc_tok)
                if KR > 256:
                    P.op("dve", lambda e: e.tensor_tensor(out=w0[0:nq], in0=hi0[0:nq], in1=lo0[0:nq], op=ALU.subtract),
                         reads=["hi0", "lo0"], writes=["w0"])
                    P.op("dve", lambda e: e.tensor_scalar(out=tt[0:nq], in0=w0[0:nq], scalar1=0.001, scalar2=1e-5,
                                                          op0=ALU.mult, op1=ALU.add), reads=["w0"], writes=["tt"])
                    P.op("dve", lambda e: e.tensor_tensor(out=lo0[0:nq], in0=lo0[0:nq], in1=tt[0:nq], op=ALU.subtract),
                         reads=["tt", "lo0"], writes=["lo0"])
                    P.op("dve", lambda e: e.tensor_tensor(out=w0[0:nq], in0=hi0[0:nq], in1=lo0[0:nq], op=ALU.subtract),
                         reads=["hi0", "lo0"], writes=["w0"])
                    P.op("dve", lambda e: e.tensor_scalar(out=steps[0:nq, :], in0=ctab[0:nq, 0:NIT + 1], scalar1=w0[0:nq],
                                                          scalar2=None, op0=ALU.mult),
                         reads=["w0", "ctab"], writes=["steps"])
                    P.op("dve", lambda e: e.tensor_tensor(out=mid[0:nq], in0=lo0[0:nq], in1=steps[0:nq, 0:1], op=ALU.add),
                         reads=["lo0", "steps"], writes=["mid"])
                    for it in range(NIT):
                        P.op("dve", lambda e: e.tensor_scalar(
                            out=junk[0:nq, 0:KR], in0=score[0:nq, 0:KR], scalar1=mid[0:nq], scalar2=None,
                            op0=ALU.is_ge, op1=ALU.add, accum_out=cnt[0:nq]),
                            reads=sc_tok + ["mid"], writes=["cnt", "junk"])
                        P.op("dve", lambda e: e.tensor_scalar(out=tt[0:nq], in0=cnt[0:nq], scalar1=255.5, scalar2=0.5,
                                                              op0=ALU.is_gt, op1=ALU.subtract),
                             reads=["cnt"], writes=["tt"])
                        P.op("dve", lambda e, it=it: e.scalar_tensor_tensor(
                            out=mid[0:nq], in0=tt[0:nq], scalar=steps[0:nq, it:it + 1], op0=ALU.mult, in1=mid[0:nq], op1=ALU.add),
                            reads=["tt", "steps", "mid"], writes=["mid"])
                    P.op("dve", lambda e: e.tensor_tensor(out=thr[0:nq], in0=mid[0:nq], in1=steps[0:nq, NIT:NIT + 1],
                                                          op=ALU.subtract),
                         reads=["mid", "steps"], writes=["thr"])
                else:
                    P.op("dve", lambda e: e.tensor_scalar(out=thr[0:nq], in0=lo0[0:nq], scalar1=1.0, scalar2=None,
                                                          op0=ALU.subtract), reads=["lo0"], writes=["thr"])
                P.op("dve", lambda e: e.tensor_scalar(
                    out=mb[0:nq, 0:KR], in0=score[0:nq, 0:KR], scalar1=thr[0:nq], scalar2=-1.0, op0=ALU.is_ge, op1=ALU.add),
                    reads=sc_tok + ["thr"], writes=[mbtok])

            def stageB(m):
                nq, q0, ti, KR = tile_params(m)
                mb = mbq[ti % 2]
                mbtok = ("mbq", ti % 2)
                ktiles = [(0, 16, 0)] + ([(1 + j, 128, 16 + 128 * j) for j in range(m + 1)] if m >= 0 else [])
                groups = [[ktiles[0]]] + [ktiles[1:][i:i + 4] for i in range(0, len(ktiles) - 1, 4)]
                items = []
                for h in range(8):
                    pb = (h % 2) * 64
                    c = h // 2
                    po = 4 + h // 4
                    hc = h % 4
                    nmm = len(ktiles)
                    imm = 0
                    for grp in groups:
                        ei = ecnt[0] % 2
                        ecnt[0] += 1
                        nkmax = grp[0][1]
                        ng = len(grp)
                        E = Eb[ei]
                        st_ = {}

                        def front(grp=grp, pb=pb, c=c, ei=ei, nkmax=nkmax, ng=ng, E=E, st_=st_):
                            pq = nextps()
                            st_["pq"] = pq
                            for gi, (kti, nk, kc0) in enumerate(grp):
                                P.op("pe", lambda e, gi=gi, nk=nk, kc0=kc0, pb=pb, c=c, pq=pq: e.matmul(
                                    ps[pq][0:nk, gi * 128:gi * 128 + nq], lhsT=akT[pb:pb + 64, c, kc0:kc0 + nk],
                                    rhs=aqT[pb:pb + 64, c, q0:q0 + nq], start=True, stop=False),
                                    reads=ALLQ, writes=[("ps", pq)], signal=False)
                                P.op("pe", lambda e, gi=gi, nk=nk, kc0=kc0, pq=pq: e.matmul(
                                    ps[pq][0:nk, gi * 128:gi * 128 + nq], lhsT=mb[0:nq, kc0:kc0 + nk],
                                    rhs=ident30k[0:nq, 0:nq], start=False, stop=True),
                                    reads=[mbtok, "i30k"], writes=[("ps", pq)], signal=(gi == ng - 1))
                            P.op("act", lambda e, E=E, pq=pq, nkmax=nkmax, ng=ng: e.activation(
                                out=E[0:nkmax, 0:ng * 128].rearrange("p (j q) -> p j q", q=128)[:, :, 0:nq],
                                in_=ps[pq][0:nkmax, 0:ng * 128].rearrange("p (j q) -> p j q", q=128)[:, :, 0:nq],
                                func=AF.Exp, scale=0.125),
                                reads=[("ps", pq)], writes=[("E", ei)])

                        def back(grp=grp, ei=ei, ng=ng, E=E, po=po, hc=hc, h=h, imm0=imm, nmm=nmm):
                            imm_ = imm0
                            for gi, (kti, nk, kc0) in enumerate(grp):
                                P.op("pe", lambda e, gi=gi, nk=nk, kti=kti, E=E, po=po, hc=hc, h=h, imm_=imm_, nmm=nmm: e.matmul(
                                    ps[po][0:nq, hc * 65:hc * 65 + 65], lhsT=E[0:nk, gi * 128:gi * 128 + nq],
                                    rhs=av[0:nk, kti, h, :], start=(imm_ == 0), stop=(imm_ == nmm - 1)),
                                    reads=[("E", ei), ("av", kti), "av1"], writes=[("ps", po)],
                                    signal=(imm_ == nmm - 1 or gi == ng - 1))
                                imm_ += 1
                        imm += len(grp)
                        items.append((front, back))
                pipeline(items)

            def stageB_tail(m):
                nq, q0, ti, KR = tile_params(m)
                for hh in range(2):
                    po = 4 + hh
                    pv = ps[po][0:nq, 0:260].rearrange("p (c e) -> p c e", e=65)
                    P.op("dve", lambda e, pv=pv, hh=hh: e.reciprocal(out=rec[0:nq, hh * 4:hh * 4 + 4], in_=pv[:, :, 64]),
                         reads=[("ps", po)], writes=[("rec", hh)])
                    P.op("dve", lambda e, pv=pv, hh=hh: e.tensor_tensor(
                        out=yatok[0:nq, hh * 256:(hh + 1) * 256].rearrange("p (c d) -> p c d", c=4),
                        in0=pv[:, :, 0:64], in1=rec[0:nq, hh * 4:hh * 4 + 4].unsqueeze(2).to_broadcast([nq, 4, 64]),
                        op=ALU.mult),
                        reads=[("ps", po), ("rec", hh)], writes=[("yatok", hh)])
                for c in range(4):
                    P.op("pe", lambda e, c=c: e.transpose(
                        out=ps7b[:, c * 128:c * 128 + nq], in_=yatok[0:nq, c * 128:(c + 1) * 128],
                        identity=ident_b[0:nq, 0:nq]),
                        reads=[("yatok", 0), ("yatok", 1), "identb"], writes=[("ps", 7)], signal=(c == 3))
                P.op("act", lambda e: e.copy(
                    out=iqT[:, :, q0:q0 + nq], in_=ps7b[:, 0:512].rearrange("p (c q) -> p c q", c=4)[:, :, 0:nq]),
                    reads=[("ps", 7)], writes=[("iqya", ti)])

            if True:
                stageI(-1)
                stageI(0)
                stageS(-1)
                for m in range(-1, 16):
                    if m + 2 <= 15:
                        stageI(m + 2)
                    if m - 1 >= -1:
                        stageB_tail(m - 1)
                    if m + 1 <= 15:
                        stageS(m + 1)
                    stageB(m)
                stageB_tail(15)
            P.barrier()
            W.release()

            P.dma("sp", "rcst0", DT, dt_d, writes=["DT"])
            P.dma("sp", "rcst1", qdecT, qdect_d, writes=["qdecT"])
            P.dma("sp", "rcst2", rc, rc_d, writes=["rc"])
            P.dma("pool", "rcst3", perm_sb, perm_d, writes=["perm"])
            pass
            wo_v = ev_out[0].rearrange("(j p) d -> p j d", p=128)
            acnt = [0]
            first_tile = [True]
            def ret_block(b):
                c0, c1 = BLKS[b]
                n = c1 - c0
                Ln = 16 if b == 0 else 128
                ntile = 1 if b == 0 else 4
                P.dma("sp", "cs", cs_sb[:, :, 0:n], cs_d[:, :, c0:c1].rearrange("a p t -> p a t"), writes=["cs"])
                norm_block(b, 16 + 0, 0, dst=xnR, dtok="xnR")
                for which, col0, raw, rot in (("bq", 2120, bqraw, bqr), ("bk", 2632, bkraw, bkr)):
                    slot, tok = W.next([(full_slot, win_v[:, :, col0:col0 + 512])])
                    items = []
                    for c in range(4):
                        def front(c=c, slot=slot, tok=tok, raw=raw, which=which):
                            pq = nextps()
                            for k in range(KC):
                                P.op("pe", lambda e, k=k, c=c, slot=slot, pq=pq: e.matmul(
                                    ps[pq][:, 0:n], lhsT=slot[:, k * 512 + c * 128:k * 512 + c * 128 + 128],
                                    rhs=xnR[:, k, 0:n], start=(k == 0), stop=(k == KC - 1)),
                                    reads=[tok, "xnR"], writes=[("ps", pq)], signal=(k == KC - 1))
                            P.op("act", lambda e, pq=pq, c=c, raw=raw: e.copy(out=raw[:, c, 0:n], in_=ps[pq][:, 0:n]),
                                 reads=[("ps", pq)], writes=[(which + "raw", c)])

                        def back(c=c, raw=raw, rot=rot, which=which):
                            pq2 = nextps()
                            P.op("pe", lambda e, pq2=pq2, c=c, raw=raw: e.matmul(
                                ps[pq2][:, 0:n], lhsT=perm_sb, rhs=raw[:, c, 0:n], start=True, stop=True),
                                reads=[(which + "raw", c), "perm"], writes=[("ps", pq2)])
                            P.op("dve", lambda e, c=c, raw=raw: e.tensor_tensor(
                                out=lnv[:, 0:n], in0=raw[:, c, 0:n], in1=cs_sb[:, 0, 0:n], op=ALU.mult),
                                reads=[(which + "raw", c), "cs"], writes=["lnv"])
                            P.op("dve", lambda e, pq2=pq2: e.tensor_tensor(
                                out=rstd[:, 0:n], in0=ps[pq2][:, 0:n], in1=cs_sb[:, 1, 0:n], op=ALU.mult),
                                reads=[("ps", pq2), "cs"], writes=["rstd"])
                            P.op("dve", lambda e, c=c, rot=rot: e.tensor_tensor(
                                out=rot[:, c, 0:n], in0=lnv[:, 0:n], in1=rstd[:, 0:n], op=ALU.add),
                                reads=["lnv", "rstd"], writes=[(which + "r", c)])
                        items.append((front, back))
                    pipeline(items)
                for which, col0, dstt in (("bv", 3144, bv_tok), ("bg", 4168, bgs_tok)):
                    for hf in range(2):
                        slot, tok = W.next([(full_slot, win_v[:, :, col0 + hf * 512:col0 + (hf + 1) * 512])])
                        for j in range(ntile):
                            pq = nextps()
                            for k in range(KC):
                                P.op("pe", lambda e, k=k, slot=slot, pq=pq, j=j: e.matmul(
                                    ps[pq][0:Ln, 0:512], lhsT=xnR[:, k, j * 128:j * 128 + Ln],
                                    rhs=slot[:, k * 512:(k + 1) * 512], start=(k == 0), stop=(k == KC - 1)),
                                    reads=[tok, "xnR"], writes=[("ps", pq)], signal=(k == KC - 1))
                            if which == "bv":
                                P.op("act", lambda e, pq=pq, j=j, hf=hf, dstt=dstt: e.copy(
                                    out=dstt[0:Ln, j, hf * 512:(hf + 1) * 512], in_=ps[pq][0:Ln, 0:512]),
                                    reads=[("ps", pq)], writes=[(which, j, hf)])
                            else:
                                P.op("act", lambda e, pq=pq, j=j, hf=hf, dstt=dstt: e.activation(
                                    out=dstt[0:Ln, j, hf * 512:(hf + 1) * 512], in_=ps[pq][0:Ln, 0:512], func=AF.Silu),
                                    reads=[("ps", pq)], writes=[(which, j, hf)])
                kdc = 8 if Ln == 16 else 0
                cdc = 20 if Ln == 16 else 16
                p2 = [0]

                def nextps2():
                    p2[0] += 1
                    return p2[0] % 2

                def tile_heads(j):
                    lo_ = j * 128
                    pob = 4 if j % 2 == 0 else 2
                    for c in range(4):
                        P.op("dve", lambda e, c=c, lo_=lo_: e.tensor_tensor(
                            out=bqd[:, c, lo_:lo_ + Ln], in0=bqr[:, c, lo_:lo_ + Ln], in1=qdecT[:, c, 0:Ln], op=ALU.mult),
                            reads=[("bqr", c), "qdecT"], writes=[("bqd", c)])
                    for c in range(4):
                        P.op("pe", lambda e, c=c, lo_=lo_: e.transpose(
                            out=ps7b[0:Ln, c * 128:(c + 1) * 128], in_=bkr[:, c, lo_:lo_ + Ln], identity=ident_b),
                            reads=[("bkr", c), "identb"], writes=[("ps", 7)], signal=(c == 3))
                    P.op("dve", lambda e: e.tensor_tensor(
                        out=kd_tok[0:Ln, :].rearrange("p (h d) -> p h d", h=8),
                        in0=ps7b[0:Ln, 0:512].rearrange("p (h d) -> p h d", h=8),
                        in1=rc[0:Ln, kdc:kdc + 8].unsqueeze(2).to_broadcast([Ln, 8, 64]), op=ALU.mult),
                        reads=[("ps", 7), "rc"], writes=["kd"])
                    items = []
                    ft = first_tile[0]
                    for h in range(8):
                        pb = (h % 2) * 64
                        c = h // 2
                        po = pob + h // 4
                        hc = h % 4
                        ai = acnt[0] % 2
                        acnt[0] += 1
                        AT = ATb[ai]

                        def front(pb=pb, c=c, lo_=lo_, ai=ai, AT=AT, h=h):
                            pq = nextps2()
                            P.op("pe", lambda e, pb=pb, c=c, pq=pq, lo_=lo_: e.matmul(
                                ps[pq][0:Ln, 0:Ln], lhsT=bkr[pb:pb + 64, c, lo_:lo_ + Ln], rhs=bqr[pb:pb + 64, c, lo_:lo_ + Ln],
                                start=True, stop=True),
                                reads=[("bkr", c), ("bqr", c)], writes=[("ps", pq)])
                            P.op("dve", lambda e, pq=pq, AT=AT, h=h: e.tensor_tensor(
                                out=AT[0:Ln, 0:Ln], in0=ps[pq][0:Ln, 0:Ln], in1=DT[0:Ln, h, 0:Ln], op=ALU.mult),
                                reads=[("ps", pq), "DT"], writes=[("AT", ai)])

                        def back(pb=pb, c=c, lo_=lo_, ai=ai, AT=AT, h=h, po=po, hc=hc, j=j, ft=ft):
                            P.op("pe", lambda e, AT=AT, po=po, hc=hc, h=h, j=j, ft=ft: e.matmul(
                                ps[po][0:Ln, hc * 128:(hc + 1) * 128], lhsT=AT[0:Ln, 0:Ln], rhs=bv_tok[0:Ln, j, h * 128:(h + 1) * 128],
                                start=True, stop=ft),
                                reads=[("AT", ai), ("bv", j, 0), ("bv", j, 1)], writes=[("ps", po)], signal=ft)
                            if not ft:
                                P.op("pe", lambda e, pb=pb, c=c, po=po, hc=hc, lo_=lo_: e.matmul(
                                    ps[po][0:Ln, hc * 128:(hc + 1) * 128], lhsT=bqd[pb:pb + 64, c, lo_:lo_ + Ln],
                                    rhs=S_bf[pb:pb + 64, c, :], start=False, stop=True),
                                    reads=[("bqd", c), "Sbf"], writes=[("ps", po)])
                            P.op("pe", lambda e, pb=pb, c=c, h=h, j=j: e.matmul(
                                ps[6][pb:pb + 64, c * 128:(c + 1) * 128], lhsT=kd_tok[0:Ln, h * 64:(h + 1) * 64],
                                rhs=bv_tok[0:Ln, j, h * 128:(h + 1) * 128], start=True, stop=True),
                                reads=["kd", ("bv", j, 0), ("bv", j, 1)], writes=[("ps", 6)], signal=(h == 7))
                        items.append((front, back))
                    pipeline(items)
                    for c in range(4):
                        if first_tile[0]:
                            P.op("dve", lambda e, c=c: e.tensor_copy(out=S_sb[:, c, :], in_=ps[6][:, c * 128:(c + 1) * 128]),
                                 reads=[("ps", 6)], writes=[("S", c)])
                        else:
                            P.op("dve", lambda e, c=c: e.scalar_tensor_tensor(
                                out=S_sb[:, c, :], in0=S_sb[:, c, :], scalar=rc[:, cdc + c:cdc + c + 1], op0=ALU.mult,
                                in1=ps[6][:, c * 128:(c + 1) * 128], op1=ALU.add),
                                reads=[("ps", 6), ("S", c), "rc"], writes=[("S", c)])
                    P.op("act", lambda e: e.copy(out=S_bf, in_=S_sb), reads=[("S", c) for c in range(4)], writes=["Sbf"])
                    if first_tile[0] and DEBUG:
                        P.op("act", lambda e: e.copy(out=sil[1], in_=ps[6]), reads=[("ps", 6)], writes=[("sil", 1)])
                        dump("ps6_first", sil[1], [("sil", 1)])
                    if first_tile[0]:
                        dump("S_first", S_sb.rearrange("p c n -> p (c n)"), [("S", c_) for c_ in range(4)])
                        dump("kd_first", kd_tok, ["kd"])
                        dump("bv_first", bv_tok[:, 0, :], [("bv", 0, 0), ("bv", 0, 1)])
                    first_tile[0] = False
                def tile_out(j):
                    lo_ = j * 128
                    pob = 4 if j % 2 == 0 else 2
                    for hh in range(2):
                        po = pob + hh
                        P.op("act", lambda e, po=po, hh=hh: e.activation(out=sil[hh][0:Ln, :], in_=ps[po][0:Ln, :], func=AF.Square),
                             reads=[("ps", po)], writes=[("sil", hh)])
                        P.op("dve", lambda e, hh=hh: e.tensor_reduce(
                            out=den[0:Ln, hh * 4:hh * 4 + 4], in_=sil[hh][0:Ln, :].rearrange("p (h e) -> p h e", h=4),
                            axis=AX.X, op=ALU.add),
                            reads=[("sil", hh)], writes=[("den", hh)])
                    P.op("act", lambda e: e.activation(out=rec[0:Ln, :], in_=den[0:Ln, :], func=AF.Ln, scale=1.0 / 128, bias=eps_sb[0:Ln]),
                         reads=[("den", 0), ("den", 1), "eps"], writes=[("rec", 0), ("rec", 1)])
                    P.op("act", lambda e: e.activation(out=rec[0:Ln, :], in_=rec[0:Ln, :], func=AF.Exp, scale=-0.5),
                         reads=[("rec", 0), ("rec", 1)], writes=[("rec", 0), ("rec", 1)])
                    for hh in range(2):
                        po = pob + hh
                        P.op("dve", lambda e, po=po, hh=hh: e.tensor_tensor(
                            out=otmp[0:Ln, hh * 512:(hh + 1) * 512].rearrange("p (h e) -> p h e", h=4),
                            in0=ps[po][0:Ln, :].rearrange("p (h e) -> p h e", h=4),
                            in1=rec[0:Ln, hh * 4:hh * 4 + 4].unsqueeze(2).to_broadcast([Ln, 4, 128]), op=ALU.mult),
                            reads=[("ps", po), ("rec", hh)], writes=["sq"], signal=True)
                    P.op("dve", lambda e, j=j: e.tensor_tensor(
                        out=yb_tok[0:Ln, :], in0=otmp[0:Ln, :], in1=bgs_tok[0:Ln, j, :], op=ALU.mult),
                        reads=["sq", ("bg", j, 0), ("bg", j, 1)], writes=["ybtok"])
                    for k in range(8):
                        P.op("pe", lambda e, k=k: e.transpose(
                            out=ps7b[:, k * 128:k * 128 + Ln], in_=yb_tok[0:Ln, k * 128:(k + 1) * 128],
                            identity=ident_b[0:Ln, 0:Ln]),
                            reads=["ybtok", "identb"], writes=[("ps", 7)], signal=(k == 7))
                    P.op("act", lambda e, lo_=lo_: e.copy(
                        out=ybT_blk[:, :, lo_:lo_ + Ln], in_=ps7b.rearrange("p (k q) -> p k q", k=8)[:, :, 0:Ln]),
                        reads=[("ps", 7)], writes=[("ybT", j)])

                for j in range(ntile):
                    tile_heads(j)
                    if j >= 1:
                        tile_out(j - 1)
                tile_out(ntile - 1)
                for dh in range(2):
                    slot, tok = W.next([(lambda slot: slot[:, 0:6144].rearrange("p (j f) -> p j f", j=12),
                                         wo_v[:, :, dh * 512:(dh + 1) * 512])])
                    for dc in range(4):
                        pq = nextps()
                        dch = dh * 4 + dc
                        js = ([0, 1, 2, 3] if EV_DSA else []) + ([4 + q_ for q_ in range(8)] if EV_RET else [])
                        for ji, j in enumerate(js):
                            rhs = iqT[:, j, c0:c1] if j < 4 else ybT_blk[:, j - 4, 0:n]
                            rtok = [("iqya", t_) for t_ in range(17)] if j < 4 else [("ybT", q_) for q_ in range(ntile)]
                            P.op("pe", lambda e, j=j, ji=ji, dc=dc, slot=slot, pq=pq, rhs=rhs, n=n, nj=len(js): e.matmul(
                                ps[pq][:, 0:n], lhsT=slot[:, j * 512 + dc * 128:j * 512 + dc * 128 + 128], rhs=rhs,
                                start=(ji == 0), stop=(ji == nj - 1)),
                                reads=[tok] + rtok, writes=[("ps", pq)], signal=(ji == len(js) - 1))
                        P.op("dve", lambda e, pq=pq, n=n, dch=dch, c0=c0, c1=c1: e.tensor_tensor(
                            out=hT[:, dch, c0:c1], in0=ps[pq][:, 0:n], in1=hT[:, dch, c0:c1], op=ALU.add),
                            reads=[("ps", pq), ("h", b, dch)], writes=[("h", b, dch)])
            for b_ in range(5):
                ret_block(b_)
            P.barrier()
            dump("bqd", bqd.rearrange("p c n -> p (c n)"), [])
            dump("DT", DT.rearrange("p c n -> p (c n)"), [])
            dump("qdecT", qdecT.rearrange("p c n -> p (c n)"), [])
            dump("rc", rc, [])
            dump("AT0", ATb[0], [])
            dump("bqraw", bqraw.rearrange("p c n -> p (c n)"), [])
            dump("bqr", bqr.rearrange("p c n -> p (c n)"), [])
            dump("bkr", bkr.rearrange("p c n -> p (c n)"), [])
            dump("S_sb", S_sb.rearrange("p c n -> p (c n)"), [])
            dump("ybT", ybT_blk.rearrange("p c n -> p (c n)"), [])
            dump("rec", rec, [])
            dump("den", den, [])
            dump("otmp", otmp, [])
            dump("kd", kd_tok, [])
            dump("ybtok", yb_tok, [])

            P.barrier()

        for layer in range(2):
            if f"f1_{layer}" in stages:
                ffn(f1_in[layer], f1_out[layer], 0 + layer * 8)
            if f"mix_{layer}" in stages and layer == 0:
                P.barrier()
                even_mixer()
            if f"mix_{layer}" in stages and layer == 1:
                P.barrier()
                odd_mixer()
            if f"f2_{layer}" in stages:
                ffn(f2_in[layer], f2_out[layer], 32 + layer * 8)
        P.barrier()

        for m in range(16):
            col0 = NM + m * 128
            buf = io[m % 2]
            for half in range(2):
                pt = ps[6 + half]
                for kk in range(4):
                    k = half * 4 + kk
                    P.op("pe", lambda e, k=k, kk=kk, pt=pt, col0=col0: e.transpose(
                        out=pt[:, kk * 128:(kk + 1) * 128], in_=hT[:, k, col0:col0 + 128], identity=ident_f),
                        reads=["identf"], writes=[("ps", 6 + half)], signal=(kk == 3))
                if half == 0:
                    P.op("dve", lambda e, pt=pt, buf=buf: e.tensor_copy(out=buf[:, 0:512], in_=pt),
                         reads=[("ps", 6)], writes=[("io", m % 2, 0)])
                else:
                    P.op("act", lambda e, pt=pt, buf=buf: e.copy(out=buf[:, 512:1024], in_=pt),
                         reads=[("ps", 7)], writes=[("io", m % 2, 1)])
            P.dma("sp", f"io{m%2}", out[m * 128:(m + 1) * 128, :], buf,
                  reads=[("io", m % 2, 0), ("io", m % 2, 1)])
        P.barrier()

    eps_sb = nc.alloc_sbuf_tensor("eps_sb", [128, 1], F32).ap()

    def prog2():
        P.op("dve", lambda e: e.memset(eps_sb, EPS), writes=["eps"])
        program()

    P.plan = True
    prog2()
    P.plan = False
    P.reset()
    W.start_real()
    prog2()
    global LAST_PROG
    LAST_PROG = P
    P.emit()
    return nc


LAST_PROG = None


def _prep_common(inputs):
    vec = np.zeros((128, 64), np.float32)
    vec[:, 48] = np.tile(inputs["od_c_q_norm"][0], 2)
    vec[:, 49] = np.tile(inputs["od_c_k_norm"][0], 2)
    vec[:, 50] = np.tile(inputs["ev_a_q_norm"][0], 2)
    vec[:, 51] = np.tile(inputs["ev_a_k_norm"][0], 2)
    vec[:, 52:56] = inputs["od_d_scale"][0].reshape(4, 128).T
    vec[:, 56:64] = np.broadcast_to(inputs["od_c_sinks"][0][None, :], (128, 8))
    invc = np.zeros((128, 64), np.float32)
    for g, w in enumerate((2, 4, 8, 16)):
        invc[:, g * 16:(g + 1) * 16] = 1.0 / np.minimum(np.arange(16) + 1, w)[None, :]
    for li in range(2):
        vec[:, 0 + li * 8:8 + li * 8] = inputs["ffn1_norm"][li].reshape(8, 128).T
        vec[:, 16 + li * 8:24 + li * 8] = inputs["mix_norm"][li].reshape(8, 128).T
        vec[:, 32 + li * 8:40 + li * 8] = inputs["ffn2_norm"][li].reshape(8, 128).T
    perm = np.concatenate([np.arange((c + 4 * hh) * 64, (c + 4 * hh) * 64 + 64) for c in range(4) for hh in range(2)])
    od_in_p = np.ascontiguousarray(inputs["od_w_in"]).copy()
    od_in_p[:, :, 0:512] = inputs["od_w_in"][:, :, perm]
    od_out_p = np.ascontiguousarray(inputs["od_w_out"]).copy()
    od_out_p[:, 0:512, :] = inputs["od_w_out"][:, perm, :]
    tpos = np.arange(T, dtype=np.float64)
    inv = 1.0 / (10000.0 ** (np.arange(0, 64, 2, dtype=np.float64) / 64.0))
    pidx = np.arange(128)
    ang = tpos[None, :] * inv[pidx % 32][:, None]
    sgn = np.where((pidx % 64) < 32, -1.0, 1.0)[:, None]
    cs_tab = np.stack([np.cos(ang), np.sin(ang) * sgn]).astype(np.float32)
    perm_tab = np.zeros((128, 128), np.float32)
    for m_ in range(128):
        perm_tab[m_ + 32 if (m_ % 64) < 32 else m_ - 32, m_] = 1.0
    gam = 1.0 - 2.0 ** (-5.0 - np.arange(8, dtype=np.float64))
    ii = np.arange(128, dtype=np.float64)
    rel = ii[None, :] - ii[:, None]
    dt_tab = np.zeros((128, 8, 128), np.float64)
    for h_ in range(8):
        dt_tab[:, h_, :] = np.where(rel >= 0, 0.125 * gam[h_] ** np.maximum(rel, 0.0), 0.0)
    qdect = np.zeros((128, 4, 128), np.float64)
    for c_ in range(4):
        for p_ in range(128):
            qdect[p_, c_, :] = gam[2 * c_ + p_ // 64] ** (ii + 1.0)
    rc_tab = np.zeros((128, 64), np.float64)
    for h_ in range(8):
        rc_tab[:, h_] = 0.125 * gam[h_] ** np.maximum(127.0 - ii, 0.0)
        rc_tab[:, 8 + h_] = 0.125 * gam[h_] ** np.maximum(15.0 - ii, 0.0)
    for c_ in range(4):
        rc_tab[:, 16 + c_] = gam[2 * c_ + pidx // 64] ** 128.0
        rc_tab[:, 20 + c_] = gam[2 * c_ + pidx // 64] ** 16.0
    com = {
        "meta_tokens": np.ascontiguousarray(inputs["meta_tokens"], np.float32),
        "ffn1_w_in": inputs["ffn1_w_in"], "ffn1_w_out": inputs["ffn1_w_out"],
        "ffn2_w_in": inputs["ffn2_w_in"], "ffn2_w_out": inputs["ffn2_w_out"],
        "vecs": vec, "ident": np.eye(128, dtype=np.float32), "invcnt": invc,
        "ctab": np.broadcast_to((0.5 ** (np.arange(32) + 1)).astype(np.float32)[None, :], (128, 32)).copy(),
        "ev_w_in": inputs["ev_w_in"], "ev_w_out": inputs["ev_w_out"],
        "cs_tab": cs_tab, "perm_tab": perm_tab, "dt_tab": dt_tab.astype(np.float32),
        "qdect_tab": qdect.astype(np.float32), "rc_tab": rc_tab.astype(np.float32),
        "od_w_in": od_in_p, "od_w_out": od_out_p, "od_d_mix": inputs["od_d_mix"],
    }
    return com


def kernel(**inputs):
    inputs = {k: np.asarray(v) for k, v in inputs.items()}
    nc = build()
    com = _prep_common(inputs)
    B = inputs["x"].shape[0]
    in_maps = []
    for b in range(B):
        m = dict(com)
        m["x"] = np.ascontiguousarray(inputs["x"][b])
        in_maps.append(m)
    res = run_bass_kernel_spmd(nc, in_maps, core_ids=list(range(B)))
    return np.stack([r["out"] for r in res.results], axis=0).astype(np.float32)
```

```python
import contextlib
import numpy as np
import concourse.bass as bass
import concourse.mybir as mybir
from concourse.bass_utils import run_bass_kernel_spmd

F32 = mybir.dt.float32
BF16 = mybir.dt.bfloat16
AF = mybir.ActivationFunctionType
ALU = mybir.AluOpType
AX = mybir.AxisListType

ENGS = ["pe", "act", "dve", "pool", "sp"]
SAME_ENG_SYNC = True

D = 1024
KC = 8
S = 2048
NM = 16
T = S + NM
DFF = 2816
FC = 22
EPS = 1e-6
BLKS = [(0, 16), (16, 528), (528, 1040), (1040, 1552), (1552, 2064)]
HALVES = [[0, 1, 2], [3, 4]]
NRING = 3
RING_ELEMS = 6144


class Prog:
    def __init__(self, nc):
        self.nc = nc
        self.plan = False
        self.reset()

    def reset(self):
        self.q = {e: [] for e in ENGS}
        self.cnt = {e: 0 for e in ENGS}
        self.waited = {e: {} for e in ENGS}
        self.lastw = {}
        self.readers = {}
        self.dcnt = {}

    def _deps(self, eng, reads, writes, skip_src=None):
        deps = set()
        for t in reads:
            if t in self.lastw:
                deps.add(self.lastw[t])
        for t in writes:
            if t in self.lastw:
                deps.add(self.lastw[t])
            for src, s in self.readers.get(t, {}).items():
                deps.add((src, s))
        for (src, s) in sorted(deps):
            if src == skip_src:
                continue
            if src == eng:
                if s > self.cnt[eng] or not SAME_ENG_SYNC or eng == "pe":
                    continue
            if self.waited[eng].get(src, 0) >= s:
                continue
            self.q[eng].append(("wait", src, s))
            self.waited[eng][src] = s

    def op(self, eng, fn, reads=(), writes=(), signal=True):
        if self.plan:
            return
        self._deps(eng, reads, writes)
        seq = self.cnt[eng] + 1
        self.q[eng].append(("op", fn, signal))
        if signal:
            self.cnt[eng] = seq
        for t in reads:
            self.readers.setdefault(t, {})[eng] = seq
        for t in writes:
            self.lastw[t] = (eng, seq)
            self.readers[t] = {}

    def dma(self, queue, dsem, out, in_, reads=(), writes=()):
        if self.plan:
            return
        src = "dma:" + dsem
        self._deps(queue, reads, writes, skip_src=src)
        seq = self.dcnt.get(src, 0) + 1
        self.dcnt[src] = seq
        self.q[queue].append(("dma", out, in_, src))
        for t in reads:
            self.readers.setdefault(t, {})[src] = seq
        for t in writes:
            self.lastw[t] = (src, seq)
            self.readers[t] = {}

    def wait_all(self, eng):
        if self.plan:
            return
        for e in ENGS:
            if e != eng and self.cnt[e] > self.waited[eng].get(e, 0):
                self.q[eng].append(("wait", e, self.cnt[e]))
                self.waited[eng][e] = self.cnt[e]
        for src, s in self.dcnt.items():
            if s > self.waited[eng].get(src, 0):
                self.q[eng].append(("wait", src, s))
                self.waited[eng][src] = s

    def barrier(self):
        for e in ENGS:
            self.wait_all(e)

    def emit(self):
        nc = self.nc
        sems = {}
        with contextlib.ExitStack() as st:
            for e in ENGS:
                sems[e] = st.enter_context(nc.semaphore("s_" + e))
            for src in self.dcnt:
                sems[src] = st.enter_context(nc.semaphore("s_" + src.replace(":", "_")))
            block = st.enter_context(nc.Block())
            handles = {"pe": block.tensor, "act": block.scalar, "dve": block.vector,
                       "pool": block.gpsimd, "sp": block.sync}

            def make(e):
                def body(engine):
                    for item in self.q[e]:
                        if item[0] == "wait":
                            _, src, s = item
                            engine.wait_ge(sems[src], s * 16 if src.startswith("dma:") else s)
                        elif item[0] == "op":
                            _, fn, signal = item
                            ins = fn(engine)
                            if signal:
                                ins.then_inc(sems[e], 1)
                        else:
                            _, out, in_, src = item
                            engine.dma_start(out=out, in_=in_).then_inc(sems[src], 16)
                return body

            for e in ENGS:
                handles[e](make(e))


class WStream:
    def __init__(self, P, ring):
        self.P = P
        self.ring = ring
        self.fills = []
        self.i = 0
        self.issued = 0
        self.fences = []
        self.released = 0

    def start_real(self):
        self.i = 0
        self.issued = 0
        self.released = 0

    def fence(self):
        if self.P.plan:
            self.fences.append(len(self.fills))

    def release(self):
        if not self.P.plan:
            self.released += 1

    def _issue(self, n):
        s = n % NRING
        for part in self.fills[n]:
            dst_fn, src = part[0], part[1]
            rd = list(part[2]) if len(part) > 2 else []
            self.P.dma("pool", f"ring{s}", dst_fn(self.ring[s]), src, reads=rd, writes=[("ring", s)])

    def next(self, parts):
        if self.P.plan:
            self.fills.append(parts)
            return self.ring[0], ("ring", 0)
        i = self.i
        self.i += 1
        lim = self.fences[self.released] if self.released < len(self.fences) else len(self.fills)
        while self.issued < min(lim, i + NRING):
            self._issue(self.issued)
            self.issued += 1
        s = i % NRING
        return self.ring[s], ("ring", s)


EV_DSA = True
EV_RET = True
DEBUG = False
DBG_MAP = {}
ALL_STAGES = ("f1_0", "mix_0", "f2_0", "f1_1", "mix_1", "f2_1")


def build(stages=ALL_STAGES):
    nc = bass.Bass("TRN2", target_bir_lowering=False)

    def din(name, shape):
        return nc.dram_tensor(name, list(shape), F32, kind="ExternalInput").ap()

    x = din("x", [S, D])
    meta = din("meta_tokens", [NM, D])
    f1_in = din("ffn1_w_in", [2, D, 2 * DFF])
    f1_out = din("ffn1_w_out", [2, DFF, D])
    f2_in = din("ffn2_w_in", [2, D, 2 * DFF])
    f2_out = din("ffn2_w_out", [2, DFF, D])
    vecs = din("vecs", [128, 64])
    ident_d = din("ident", [128, 128])
    invcnt_d = din("invcnt", [128, 64])
    ev_in = din("ev_w_in", [1, D, 5192])
    ev_out = din("ev_w_out", [1, 1536, D])
    ctab_d = din("ctab", [128, 32])
    cs_d = din("cs_tab", [2, 128, T])
    perm_d = din("perm_tab", [128, 128])
    dt_d = din("dt_tab", [128, 8, 128])
    qdect_d = din("qdect_tab", [128, 4, 128])
    rc_d = din("rc_tab", [128, 64])
    od_in = din("od_w_in", [1, D, 1280])
    od_out = din("od_w_out", [1, D, D])
    od_dmix = din("od_d_mix", [1, 4, 128, 128])
    out = nc.dram_tensor("out", [S, D], F32, kind="ExternalOutput").ap()
    dbg = nc.dram_tensor("dbg", [128, 24576], F32, kind="ExternalOutput").ap() if DEBUG else None
    dbg_pos = [0]
    dbg_map = {}

    def dump(name, ap2d, toks):
        if not DEBUG or P.plan:
            return
        nn = ap2d.shape[1]
        c0_ = dbg_pos[0]
        dbg_pos[0] += nn
        dbg_map[name] = (c0_, nn, ap2d.shape[0])
        P.dma("pool", "dbg", dbg[0:ap2d.shape[0], c0_:c0_ + nn], ap2d, reads=toks)
    global DBG_MAP
    DBG_MAP = dbg_map

    P = Prog(nc)

    hT = nc.alloc_sbuf_tensor("hT", [128, KC, T], F32).ap()
    vec_sb = nc.alloc_sbuf_tensor("vec_sb", [128, 64], F32).ap()
    invcnt = nc.alloc_sbuf_tensor("invcnt_sb", [128, 64], F32).ap()
    esink = nc.alloc_sbuf_tensor("esink", [128, 8], F32).ap()
    ctab = nc.alloc_sbuf_tensor("ctab_sb", [128, 32], F32).ap()
    half_c = nc.alloc_sbuf_tensor("half_c", [128, 1], F32).ap()
    bvec = nc.alloc_sbuf_tensor("bvec", [128, 48], F32).ap()
    den = nc.alloc_sbuf_tensor("den", [128, 8], F32).ap()
    rec = nc.alloc_sbuf_tensor("rec", [128, 8], F32).ap()
    bd_ones = nc.alloc_sbuf_tensor("bd_ones", [128, 128], BF16).ap()
    dmix_sb = nc.alloc_sbuf_tensor("dmix_sb", [128, 4, 128], BF16).ap()
    ident_f = nc.alloc_sbuf_tensor("ident_f", [128, 128], F32).ap()
    ident_b = nc.alloc_sbuf_tensor("ident_b", [128, 128], BF16).ap()
    ones_b = nc.alloc_sbuf_tensor("ones_b", [128, 128], BF16).ap()
    ring = [nc.alloc_sbuf_tensor(f"ring{i}", [128, RING_ELEMS], BF16).ap() for i in range(NRING)]
    xn = nc.alloc_sbuf_tensor("xn", [128, KC, 1040], BF16).ap()
    sq = nc.alloc_sbuf_tensor("sq", [128, KC, 512], BF16).ap()
    lnv = nc.alloc_sbuf_tensor("lnv", [128, 512], F32).ap()
    rstd = nc.alloc_sbuf_tensor("rstd", [128, 512], F32).ap()
    sil = [nc.alloc_sbuf_tensor(f"sil{i}", [128, 512], F32).ap() for i in range(2)]
    ARENA_B = 73000
    arena = nc.alloc_sbuf_tensor("arena", [128, ARENA_B // 2], BF16).ap()

    def carve(off, shape, dt, base=None):
        base = arena if base is None else base
        nel = int(np.prod(shape))
        nb = nel * (4 if dt == F32 else 2)
        assert off % 4 == 0 and off + nb <= base.shape[1] * 2, (off, shape)
        v = base[:, off // 2:(off + nb) // 2]
        if dt == F32:
            v = v.bitcast(F32)
        if len(shape) == 2:
            return v.rearrange("p (a b) -> p a b", a=shape[0])
        if len(shape) == 3:
            return v.rearrange("p (a b c) -> p a b c", a=shape[0], b=shape[1])
        return v

    aT = carve(0, [FC, 1040], BF16)
    io = [carve(45760 + i * 4096, [D], F32) for i in range(2)]
    cqT = carve(0, [4, T], BF16)
    ckT = carve(16512, [T], BF16)
    cv = carve(20640, [17, 2, 65], BF16)
    dxT = carve(25088, [4, T], F32)
    akT = carve(0, [4, T], BF16)
    aqT = carve(16512, [4, T], BF16)
    iqT = carve(33024, [4, T], BF16)
    ikT2 = carve(49536, [T], BF16)
    av = carve(53664, [17, 8, 65], BF16)
    iwabs = carve(71344, [17, 8], F32)
    iwsgn = carve(71888, [17, 8], F32)
    xn_flat = xn.rearrange("p k n -> p (k n)")
    score_b = [carve(0, [T], F32, base=xn_flat), carve(0, [T], F32, base=ring[0])]
    mbq = [carve(8256 + i * 4128, [T], BF16, base=xn_flat) for i in range(2)]
    ptmp = [carve(i * 2112, [528], F32, base=xn_flat) for i in range(2)]
    yg = carve(4224, [4, 512], BF16, base=xn_flat)
    ydT = carve(8320, [4, 512], BF16, base=xn_flat)
    ycT = carve(12416, [4, 512], BF16, base=xn_flat)
    bqraw = carve(0, [4, 512], BF16)
    bkraw = carve(4096, [4, 512], BF16)
    bqr = carve(8192, [4, 512], BF16)
    bkr = carve(12288, [4, 512], BF16)
    bqd = carve(16384, [4, 512], BF16)
    bv_tok = carve(20480, [4, 1024], BF16)
    cs_sb = carve(28672, [2, 512], F32)
    bgs_tok = carve(49536, [4, 1024], BF16)
    ybT_blk = carve(57728, [8, 512], BF16)
    kd_tok = carve(65920, [512], BF16)
    ATb = [carve(66944 + i * 256, [128], BF16) for i in range(2)]
    S_sb = carve(67456, [4, 128], F32)
    S_bf = carve(69504, [4, 128], BF16)
    yb_tok = carve(70528, [1024], BF16)
    xnR = carve(0, [8, 512], BF16, base=xn_flat)
    DT = carve(8192, [8, 128], F32, base=xn_flat)
    qdecT = carve(12288, [4, 128], F32, base=xn_flat)
    perm_sb = carve(14336, [128], BF16, base=xn_flat)
    rc = carve(14592, [64], F32, base=xn_flat)
    sq_flat = sq.rearrange("p k n -> p (k n)")
    otmp = carve(0, [1024], BF16, base=sq_flat)
    junk = carve(0, [T], BF16, base=sq_flat)
    Eb = [carve(4352 + i * 1024, [512], BF16, base=sq_flat) for i in range(2)]
    ident30k = carve(0, [128], BF16, base=lnv.bitcast(BF16))
    Dh = carve(0, [8, 128], BF16, base=rstd.bitcast(BF16))
    relub = [carve(j_ * 1024, [512], BF16, base=sil[i_].bitcast(BF16)) for i_ in range(2) for j_ in range(2)]
    yatok = carve(1024, [512], BF16, base=lnv.bitcast(BF16))
    Et = [carve(i * 768, [384], BF16, base=sq_flat) for i in range(2)]
    yctok = carve(1536, [512], BF16, base=sq_flat)
    sqh = carve(2560, [512], BF16, base=sq_flat)
    sqh2 = [sqh, carve(3584, [512], BF16, base=sq_flat)]
    ps = [nc.alloc_psum_tensor(f"ps{i}", [128, 512], F32).ap() for i in range(8)]

    W = WStream(P, ring)

    def program():
        P.dma("sp", "cst0", vec_sb, vecs, writes=["vec"])
        P.dma("sp", "cst1", ident_f, ident_d, writes=["identf"])
        P.op("dve", lambda e: e.tensor_copy(out=ident_b, in_=ident_f), reads=["identf"], writes=["identb"])
        P.op("dve", lambda e: e.memset(ones_b, 1.0), writes=["ones"])
        P.dma("sp", "cst2", invcnt, invcnt_d, writes=["invcnt"])
        P.dma("sp", "cst4", ctab, ctab_d, writes=["ctab"])
        P.dma("pool", "cst3", dmix_sb, od_dmix[0].rearrange("g c e -> c g e"), writes=["dmix"])
        P.op("pool", lambda e: e.memset(bd_ones, 0.0), writes=["bd"], signal=False)
        P.op("pool", lambda e: e.memset(bd_ones[0:64, 0:64], 1.0), writes=["bd"], signal=False)
        P.op("pool", lambda e: e.memset(bd_ones[64:128, 64:128], 1.0), writes=["bd"])

        def load_tile(ti, src_ap, nrow, col0):
            buf = io[ti % 2]
            P.dma("sp", f"io{ti%2}", buf[0:nrow, :], src_ap, writes=[("io", ti % 2)])
            for half in range(2):
                pt = ps[6 + half]
                for kk in range(4):
                    k = half * 4 + kk
                    P.op("pe", lambda e, k=k, kk=kk, pt=pt, buf=buf: e.transpose(
                        out=pt[:, kk * 128: kk * 128 + nrow], in_=buf[0:nrow, k * 128:(k + 1) * 128],
                        identity=ident_f[0:nrow, 0:nrow]),
                        reads=[("io", ti % 2), "identf"], writes=[("ps", 6 + half)], signal=(kk == 3))
                P.op("dve" if half == 0 else "act",
                     (lambda e, pt=pt, half=half: e.tensor_copy(
                         out=hT[:, half * 4:(half + 1) * 4, col0:col0 + nrow],
                         in_=pt.rearrange("p (k n) -> p k n", k=4)[:, :, 0:nrow])) if half == 0 else
                     (lambda e, pt=pt, half=half: e.copy(
                         out=hT[:, half * 4:(half + 1) * 4, col0:col0 + nrow],
                         in_=pt.rearrange("p (k n) -> p k n", k=4)[:, :, 0:nrow])),
                     reads=[("ps", 6 + half)], writes=[("hld", ti, half)])

        load_tile(0, meta, NM, 0)
        for m in range(16):
            load_tile(m + 1, x[m * 128:(m + 1) * 128, :], 128, NM + m * 128)
        P.barrier()

        def norm_block(b, gcol, xoff, dst=None, dtok=None, rng=None, hname="h"):
            dst = xn if dst is None else dst
            dtok = ("xn", xoff) if dtok is None else dtok
            c0, c1 = BLKS[b] if rng is None else rng
            n = c1 - c0
            hk = [(hname, b, k) for k in range(KC)]
            P.op("act", lambda e: e.activation(out=sq[:, :, 0:n], in_=hT[:, :, c0:c1], func=AF.Square),
                 reads=hk, writes=["sq"])
            for k in range(KC):
                P.op("pe", lambda e, k=k: e.matmul(ps[6][:, 0:n], lhsT=ones_b, rhs=sq[:, k, 0:n],
                                                   start=(k == 0), stop=(k == KC - 1)),
                     reads=["sq", "ones"], writes=[("ps", 6)], signal=(k == KC - 1))
            P.op("act", lambda e: e.activation(out=lnv[:, 0:n], in_=ps[6][:, 0:n], func=AF.Ln,
                                               scale=1.0 / D, bias=eps_sb),
                 reads=[("ps", 6), "eps"], writes=["lnv"])
            P.op("act", lambda e: e.activation(out=rstd[:, 0:n], in_=lnv[:, 0:n], func=AF.Exp, scale=-0.5),
                 reads=["lnv"], writes=["rstd"])
            for k in range(KC):
                P.op("dve", lambda e, k=k: e.scalar_tensor_tensor(
                    out=dst[:, k, xoff:xoff + n], in0=hT[:, k, c0:c1], scalar=vec_sb[:, gcol + k:gcol + k + 1],
                    op0=ALU.mult, in1=rstd[:, 0:n], op1=ALU.mult),
                    reads=[(hname, b, k), "rstd", "vec"], writes=[dtok], signal=(k == KC - 1))

        def ffn(w_in, w_out, gcol):
            win_v = w_in.rearrange("(k p) f -> p k f", p=128)
            wout_v = w_out.rearrange("(j p) d -> p j d", p=128)
            FB = [(i * 344, (i + 1) * 344) for i in range(6)]
            BLKS = FB
            for half in ([0, 1, 2], [3, 4, 5]):
                xoffs = {}
                for b in half:
                    xoffs[b] = (b % 3) * 344
                    norm_block(b, gcol, xoffs[b], rng=FB[b], hname="hf")
                cnt = 0
                for fg in range(FC // 2):
                    def dg(slot):
                        return slot[:, 0:4096].rearrange("p (k t c) -> p k t c", k=8, t=2, c=256)[:, :, 0, :]

                    def du(slot):
                        return slot[:, 0:4096].rearrange("p (k t c) -> p k t c", k=8, t=2, c=256)[:, :, 1, :]
                    slot, tok = W.next([(dg, win_v[:, :, fg * 256:(fg + 1) * 256]),
                                        (du, win_v[:, :, DFF + fg * 256:DFF + (fg + 1) * 256])])
                    for b in half:
                        n = BLKS[b][1] - BLKS[b][0]
                        xo = xoffs[b]
                        for j in range(2):
                            pg = cnt % 2
                            pu = 2 + cnt % 2
                            cnt += 1
                            for k in range(KC):
                                P.op("pe", lambda e, k=k, j=j, slot=slot, pg=pg, xo=xo, n=n: e.matmul(
                                    ps[pg][:, 0:n], lhsT=slot[:, k * 512 + j * 128:k * 512 + j * 128 + 128],
                                    rhs=xn[:, k, xo:xo + n], start=(k == 0), stop=(k == KC - 1)),
                                    reads=[tok, ("xn", xo)], writes=[("ps", pg)], signal=(k == KC - 1))
                            for k in range(KC):
                                P.op("pe", lambda e, k=k, j=j, slot=slot, pu=pu, xo=xo, n=n: e.matmul(
                                    ps[pu][:, 0:n], lhsT=slot[:, k * 512 + 256 + j * 128:k * 512 + 256 + j * 128 + 128],
                                    rhs=xn[:, k, xo:xo + n], start=(k == 0), stop=(k == KC - 1)),
                                    reads=[tok, ("xn", xo)], writes=[("ps", pu)], signal=(k == KC - 1))
                            sb = sil[pg]
                            P.op("act", lambda e, pg=pg, n=n, sb=sb: e.activation(out=sb[:, 0:n], in_=ps[pg][:, 0:n], func=AF.Silu),
                                 reads=[("ps", pg)], writes=[("sil", pg)])
                            fch = fg * 2 + j
                            P.op("dve", lambda e, sb=sb, pu=pu, n=n, fch=fch, xo=xo: e.tensor_tensor(
                                out=aT[:, fch, xo:xo + n], in0=sb[:, 0:n], in1=ps[pu][:, 0:n], op=ALU.mult),
                                reads=[("sil", pg), ("ps", pu)], writes=[("aT", fch, xo)])
                cnt = 0
                for dgi in range(4):
                    def do(slot):
                        return slot[:, 0:FC * 256].rearrange("p (j c) -> p j c", j=FC, c=256)
                    slot, tok = W.next([(do, wout_v[:, :, dgi * 256:(dgi + 1) * 256])])
                    for b in half:
                        n = BLKS[b][1] - BLKS[b][0]
                        xo = xoffs[b]
                        c0, c1 = BLKS[b]
                        for c in range(2):
                            py = 4 + cnt % 2
                            cnt += 1
                            dch = dgi * 2 + c
                            for j in range(FC):
                                P.op("pe", lambda e, j=j, c=c, slot=slot, py=py, xo=xo, n=n: e.matmul(
                                    ps[py][:, 0:n], lhsT=slot[:, j * 256 + c * 128:j * 256 + c * 128 + 128],
                                    rhs=aT[:, j, xo:xo + n], start=(j == 0), stop=(j == FC - 1)),
                                    reads=[tok, ("aT", j, xo)], writes=[("ps", py)], signal=(j == FC - 1))
                            P.op("dve", lambda e, py=py, n=n, dch=dch, c0=c0, c1=c1: e.scalar_tensor_tensor(
                                out=hT[:, dch, c0:c1], in0=ps[py][:, 0:n], scalar=0.5, op0=ALU.mult,
                                in1=hT[:, dch, c0:c1], op1=ALU.add),
                                reads=[("ps", py), ("hf", b, dch)], writes=[("hf", b, dch)])


        XO = {0: 0, 1: 16, 2: 528, 3: 16, 4: 528}

        def pipeline(items):
            if not items:
                return
            items[0][0]()
            for i_ in range(len(items)):
                if i_ + 1 < len(items):
                    items[i_ + 1][0]()
                items[i_][1]()

        ps7b = ps[7].bitcast(BF16)

        hn_cnt = [0]

        def hn_front(psrc, n, src_tok, i):
            P.op("act", lambda e: e.activation(out=sqh2[i][:, 0:n], in_=psrc[:, 0:n], func=AF.Square),
                 reads=[src_tok], writes=[("sqh", i)])

        def hn_back(psrc, n, gcol, dst, dst_tok, src_tok, i):
            P.op("pe", lambda e: e.matmul(ps[6][:, 0:n], lhsT=bd_ones, rhs=sqh2[i][:, 0:n], start=True, stop=True),
                 reads=[("sqh", i), "bd"], writes=[("ps", 6)])
            P.op("act", lambda e: e.activation(out=lnv[:, 0:n], in_=ps[6][:, 0:n], func=AF.Ln,
                                               scale=1.0 / 64, bias=eps_sb),
                 reads=[("ps", 6), "eps"], writes=["lnv"])
            P.op("act", lambda e: e.activation(out=rstd[:, 0:n], in_=lnv[:, 0:n], func=AF.Exp, scale=-0.5),
                 reads=["lnv"], writes=["rstd"])
            P.op("dve", lambda e: e.scalar_tensor_tensor(
                out=dst, in0=psrc[:, 0:n], scalar=vec_sb[:, gcol:gcol + 1], op0=ALU.mult,
                in1=rstd[:, 0:n], op1=ALU.mult),
                reads=[src_tok, "rstd", "vec"], writes=[dst_tok])

        def headnorm(psrc, n, gcol, dst, dst_tok, src_tok):
            i = hn_cnt[0] % 2
            hn_cnt[0] += 1
            hn_front(psrc, n, src_tok, i)
            hn_back(psrc, n, gcol, dst, dst_tok, src_tok, i)

        def proj_headnorm_items(slot, tok, blocks, nextps_fn, gcol, dstT, which, ncw=512):
            items = []
            for b in blocks:
                c0, c1 = BLKS[b]
                n = c1 - c0
                xo = XO[b]
                for c in range(4):
                    i = hn_cnt[0] % 2
                    hn_cnt[0] += 1
                    st_ = {}

                    def front(c=c, n=n, xo=xo, i=i, st_=st_):
                        pq = nextps_fn()
                        st_["pq"] = pq
                        for k in range(KC):
                            P.op("pe", lambda e, k=k, c=c, pq=pq, xo=xo, n=n: e.matmul(
                                ps[pq][:, 0:n], lhsT=slot[:, k * ncw + c * 128:k * ncw + c * 128 + 128],
                                rhs=xn[:, k, xo:xo + n], start=(k == 0), stop=(k == KC - 1)),
                                reads=[tok, ("xn", xo)], writes=[("ps", pq)], signal=(k == KC - 1))
                        hn_front(ps[pq], n, ("ps", pq), i)

                    def back(c=c, n=n, i=i, st_=st_, c0=c0, c1=c1, b=b):
                        pq = st_["pq"]
                        hn_back(ps[pq], n, gcol, dstT[:, c, c0:c1], (which, c, b), ("ps", pq), i)
                    items.append((front, back))
            return items

        def odd_mixer():
            win_v = od_in[0].rearrange("(k p) f -> p k f", p=128)
            P.op("pool", lambda e: e.memset(cv[:, :, :, 64:65], 1.0), writes=["cv1"])
            P.op("act", lambda e: e.activation(out=esink, in_=vec_sb[:, 56:64], func=AF.Exp),
                 reads=["vec"], writes=["esink"])
            pcnt = [0]

            def nextps():
                pcnt[0] += 1
                return pcnt[0] % 4

            for half in HALVES:
                for b in half:
                    norm_block(b, 16 + 8, XO[b])
                slot, tok = W.next([(lambda slot: slot[:, 0:4096].rearrange("p (k f) -> p k f", k=8), win_v[:, :, 0:512])])
                pipeline(proj_headnorm_items(slot, tok, half, nextps, 48, cqT, "cq"))
                slot, tok = W.next([(lambda slot: slot[:, 0:2048].rearrange("p (k f) -> p k f", k=8), win_v[:, :, 512:768])])
                for b in half:
                    c0, c1 = BLKS[b]
                    n = c1 - c0
                    xo = XO[b]
                    pq = nextps()
                    for k in range(KC):
                        P.op("pe", lambda e, k=k, slot=slot, pq=pq, xo=xo, n=n: e.matmul(
                            ps[pq][:, 0:n], lhsT=slot[:, k * 256:k * 256 + 128],
                            rhs=xn[:, k, xo:xo + n], start=(k == 0), stop=(k == KC - 1)),
                            reads=[tok, ("xn", xo)], writes=[("ps", pq)], signal=(k == KC - 1))
                    headnorm(ps[pq], n, 49, ckT[:, c0:c1], ("ck", b), ("ps", pq))
                    ntile = 1 if b == 0 else 4
                    for j in range(ntile):
                        nt = 16 if b == 0 else 128
                        ti = 0 if b == 0 else 1 + (b - 1) * 4 + j
                        pq = nextps()
                        for k in range(KC):
                            P.op("pe", lambda e, k=k, slot=slot, pq=pq, xo=xo, j=j, nt=nt: e.matmul(
                                ps[pq][0:nt, 0:128], lhsT=xn[:, k, xo + j * 128:xo + j * 128 + nt],
                                rhs=slot[:, k * 256 + 128:k * 256 + 256], start=(k == 0), stop=(k == KC - 1)),
                                reads=[tok, ("xn", xo)], writes=[("ps", pq)], signal=(k == KC - 1))
                        P.op("act", lambda e, pq=pq, nt=nt, ti=ti: e.copy(
                            out=cv[0:nt, ti, :, 0:64], in_=ps[pq][0:nt, 0:128].rearrange("p (g d) -> p g d", g=2)),
                            reads=[("ps", pq)], writes=[("cv", ti)])
                slot, tok = W.next([(lambda slot: slot[:, 0:4096].rearrange("p (k f) -> p k f", k=8), win_v[:, :, 768:1280])])
                for b in half:
                    c0, c1 = BLKS[b]
                    n = c1 - c0
                    xo = XO[b]
                    for g in range(4):
                        pq = nextps()
                        for k in range(KC):
                            P.op("pe", lambda e, k=k, g=g, slot=slot, pq=pq, xo=xo, n=n: e.matmul(
                                ps[pq][:, 0:n], lhsT=slot[:, k * 512 + g * 128:k * 512 + g * 128 + 128],
                                rhs=xn[:, k, xo:xo + n], start=(k == 0), stop=(k == KC - 1)),
                                reads=[tok, ("xn", xo)], writes=[("ps", pq)], signal=(k == KC - 1))
                        P.op("act", lambda e, pq=pq, g=g, c0=c0, c1=c1, n=n: e.copy(out=dxT[:, g, c0:c1], in_=ps[pq][:, 0:n]),
                             reads=[("ps", pq)], writes=[("dx", g, b)])
            P.barrier()

            ecnt = [0]
            for b in range(5):
                c0, c1 = BLKS[b]
                n = c1 - c0
                tiles = [(-1, 16, 0)] if b == 0 else [(4 * (b - 1) + j, 128, 16 + 128 * (4 * (b - 1) + j)) for j in range(4)]
                for (m, nq, q0) in tiles:
                    qb = 0 if m < 0 else (m - 4 * (b - 1)) * 128
                    items = []
                    for h in range(8):
                        c = h % 4
                        hh = h // 4
                        pb = hh * 64
                        ei = ecnt[0] % 2
                        ecnt[0] += 1
                        E = Et[ei]
                        rq = cqT[pb:pb + 64, c, q0:q0 + nq]
                        segs = [(0, 16, 0, 0)]
                        if m >= 1:
                            segs.append((1, 128, 16 + 128 * (m - 1), m))
                        if m >= 0:
                            segs.append((2, 128, 16 + 128 * m, 1 + m))

                        def front(c=c, pb=pb, ei=ei, E=E, rq=rq, segs=segs, m=m, nq=nq):
                            pq = nextps()
                            rd = [("cq", c, bb) for bb in range(5)] + [("ck", bb) for bb in range(5)]
                            for si, (sg, nk, k0, _) in enumerate(segs):
                                P.op("pe", lambda e, sg=sg, nk=nk, k0=k0, pb=pb, pq=pq, rq=rq, nq=nq: e.matmul(
                                    ps[pq][0:nk, sg * 128:sg * 128 + nq], lhsT=ckT[pb:pb + 64, k0:k0 + nk], rhs=rq,
                                    start=True, stop=True),
                                    reads=rd, writes=[("ps", pq)], signal=(si == len(segs) - 1))
                            P.op("act", lambda e, E=E, pq=pq, nq=nq: e.activation(
                                out=E[0:16, 0:nq], in_=ps[pq][0:16, 0:nq], func=AF.Exp, scale=0.125),
                                reads=[("ps", pq)], writes=[("E", ei)], signal=(m < 0))
                            if m >= 0:
                                lo = 128 if m >= 1 else 256
                                P.op("act", lambda e, E=E, pq=pq, lo=lo: e.activation(
                                    out=E[:, lo:384], in_=ps[pq][:, lo:384], func=AF.Exp, scale=0.125),
                                    reads=[("ps", pq)], writes=[("E", ei)])
                                if m >= 1:
                                    P.op("pool", lambda e, E=E: e.memset(E[0:64, 192:256], 0.0),
                                         reads=[], writes=[("E", ei)], signal=False)
                                P.op("pool", lambda e, E=E: e.memset(E[64:128, 256:320], 0.0),
                                     reads=[], writes=[("E", ei)])

                        def back(c=c, hh=hh, ei=ei, E=E, segs=segs, nq=nq):
                            po = 4 + hh
                            for si, (sg, nk, k0, ti) in enumerate(segs):
                                P.op("pe", lambda e, sg=sg, nk=nk, ti=ti, E=E, po=po, c=c, hh=hh, nq=nq, si=si, ns=len(segs): e.matmul(
                                    ps[po][0:nq, c * 65:c * 65 + 65], lhsT=E[0:nk, sg * 128:sg * 128 + nq],
                                    rhs=cv[0:nk, ti, hh, :], start=(si == 0), stop=(si == ns - 1)),
                                    reads=[("E", ei), ("cv", ti), "cv1"], writes=[("ps", po)], signal=(si == len(segs) - 1))
                        items.append((front, back))
                    pipeline(items)
                    for hh in range(2):
                        po = 4 + hh
                        pv = ps[po][0:nq, 0:260].rearrange("p (c e) -> p c e", e=65)
                        P.op("dve", lambda e, pv=pv, hh=hh, nq=nq: e.tensor_tensor(
                            out=den[0:nq, hh * 4:hh * 4 + 4], in0=pv[:, :, 64], in1=esink[0:nq, hh * 4:hh * 4 + 4], op=ALU.add),
                            reads=[("ps", po), "esink"], writes=[("den", hh)])
                        P.op("dve", lambda e, hh=hh, nq=nq: e.reciprocal(out=rec[0:nq, hh * 4:hh * 4 + 4], in_=den[0:nq, hh * 4:hh * 4 + 4]),
                             reads=[("den", hh)], writes=[("rec", hh)])
                        P.op("dve", lambda e, pv=pv, hh=hh, nq=nq: e.tensor_tensor(
                            out=yctok[0:nq, :].rearrange("p (c x d) -> p c x d", c=4, x=2)[:, :, hh, :],
                            in0=pv[:, :, 0:64], in1=rec[0:nq, hh * 4:hh * 4 + 4].unsqueeze(2).to_broadcast([nq, 4, 64]),
                            op=ALU.mult),
                            reads=[("ps", po), ("rec", hh)], writes=[("yctok", hh)])
                    for c in range(4):
                        P.op("pe", lambda e, c=c, nq=nq: e.transpose(
                            out=ps7b[:, c * 128:c * 128 + nq], in_=yctok[0:nq, c * 128:(c + 1) * 128],
                            identity=ident_b[0:nq, 0:nq]),
                            reads=[("yctok", 0), ("yctok", 1), "identb"], writes=[("ps", 7)], signal=(c == 3))
                    P.op("act", lambda e, nq=nq, qb=qb: e.copy(
                        out=ycT[:, :, qb:qb + nq], in_=ps7b[:, 0:512].rearrange("p (c q) -> p c q", c=4)[:, :, 0:nq]),
                        reads=[("ps", 7)], writes=[("ycT", qb)])
                LB = 0 if b == 0 else 16
                Wd = n + LB
                for g, w in enumerate((2, 4, 8, 16)):
                    src = dxT[:, g, c0 - LB:c1]
                    cur = src
                    cur_tok = [("dx", g, bb) for bb in range(5)]
                    for lv in range(g + 1):
                        sh = 1 << lv
                        dst = ptmp[lv % 2]
                        P.op("pool", lambda e, dst=dst, cur=cur, sh=sh: e.tensor_copy(out=dst[:, 0:sh], in_=cur[:, 0:sh]),
                             reads=cur_tok, writes=[("ptmp", lv % 2)], signal=False)
                        P.op("pool", lambda e, dst=dst, cur=cur, sh=sh, Wd=Wd: e.tensor_tensor(
                            out=dst[:, sh:Wd], in0=cur[:, sh:Wd], in1=cur[:, 0:Wd - sh], op=ALU.add),
                            reads=cur_tok, writes=[("ptmp", lv % 2)])
                        cur = dst
                        cur_tok = [("ptmp", lv % 2)]
                    if b == 0:
                        P.op("dve", lambda e, cur=cur, g=g: e.tensor_tensor(
                            out=lnv[:, 0:16], in0=cur[:, 0:16], in1=invcnt[:, g * 16:(g + 1) * 16], op=ALU.mult),
                            reads=cur_tok + ["invcnt"], writes=["lnv"])
                        P.op("dve", lambda e, g=g: e.tensor_tensor(
                            out=yg[:, g, 0:16], in0=lnv[:, 0:16], in1=dxT[:, g, 0:16], op=ALU.subtract),
                            reads=["lnv", ("dx", g, 0)], writes=[("yg", g)])
                    else:
                        P.op("dve", lambda e, cur=cur, g=g, w=w, n=n, c0=c0, c1=c1: e.scalar_tensor_tensor(
                            out=yg[:, g, 0:n], in0=cur[:, 16:16 + n], scalar=1.0 / w, op0=ALU.mult,
                            in1=dxT[:, g, c0:c1], op1=ALU.subtract),
                            reads=cur_tok + [("dx", g, b)], writes=[("yg", g)])
                    pq = nextps()
                    P.op("pe", lambda e, g=g, pq=pq, n=n: e.matmul(ps[pq][:, 0:n], lhsT=dmix_sb[:, g, :], rhs=yg[:, g, 0:n],
                                                                 start=True, stop=True),
                         reads=[("yg", g), "dmix"], writes=[("ps", pq)])
                    P.op("dve", lambda e, g=g, pq=pq, n=n: e.tensor_scalar(
                        out=ydT[:, g, 0:n], in0=ps[pq][:, 0:n], scalar1=vec_sb[:, 52 + g:53 + g], scalar2=None, op0=ALU.mult),
                        reads=[("ps", pq), "vec"], writes=[("ydT", g)])
                wo = od_out[0]
                for dh in range(2):
                    slot, tok = W.next([(lambda slot: slot[:, 0:4096].rearrange("p (k f) -> p k f", k=8),
                                         wo.rearrange("(j p) d -> p j d", p=128)[:, :, dh * 512:(dh + 1) * 512])])
                    for dc in range(4):
                        pq = nextps()
                        dch = dh * 4 + dc
                        for j in range(8):
                            rhs = ycT[:, j, 0:n] if j < 4 else ydT[:, j - 4, 0:n]
                            rtok = [("ycT", qq) for qq in (0, 128, 256, 384)] if j < 4 else [("ydT", j - 4)]
                            P.op("pe", lambda e, j=j, dc=dc, slot=slot, pq=pq, rhs=rhs, n=n: e.matmul(
                                ps[pq][:, 0:n], lhsT=slot[:, j * 512 + dc * 128:j * 512 + dc * 128 + 128], rhs=rhs,
                                start=(j == 0), stop=(j == 7)),
                                reads=[tok] + rtok, writes=[("ps", pq)], signal=(j == 7))
                        P.op("dve", lambda e, pq=pq, n=n, dch=dch, c0=c0, c1=c1: e.tensor_tensor(
                            out=hT[:, dch, c0:c1], in0=ps[pq][:, 0:n], in1=hT[:, dch, c0:c1], op=ALU.add),
                            reads=[("ps", pq), ("h", b, dch)], writes=[("h", b, dch)])
            P.barrier()


        IWS = (8 ** -0.5) * (64 ** -0.5)
        NIT = 20

        def even_mixer(do_ret=True):
            win_v = ev_in[0].rearrange("(k p) f -> p k f", p=128)
            P.op("pool", lambda e: e.memset(av[:, :, :, 64:65], 1.0), writes=["av1"])
            P.op("pool", lambda e: e.memset(half_c, 0.5), writes=["halfc"])
            pcnt = [0]

            def nextps():
                pcnt[0] += 1
                return pcnt[0] % 4

            def full_slot(slot):
                return slot[:, 0:4096].rearrange("p (k f) -> p k f", k=8)

            for half in HALVES:
                for b in half:
                    norm_block(b, 16 + 0, XO[b])
                for which, col0, gcol, dstT in (("aq", 0, 50, aqT), ("ak", 512, 51, akT), ("iq", 1536, None, iqT)):
                    slot, tok = W.next([(full_slot, win_v[:, :, col0:col0 + 512])])
                    if gcol is not None:
                        pipeline(proj_headnorm_items(slot, tok, half, nextps, gcol, dstT, which))
                        continue
                    for b in half:
                        c0, c1 = BLKS[b]
                        n = c1 - c0
                        xo = XO[b]
                        for c in range(4):
                            pq = nextps()
                            for k in range(KC):
                                P.op("pe", lambda e, k=k, c=c, slot=slot, pq=pq, xo=xo, n=n: e.matmul(
                                    ps[pq][:, 0:n], lhsT=slot[:, k * 512 + c * 128:k * 512 + c * 128 + 128],
                                    rhs=xn[:, k, xo:xo + n], start=(k == 0), stop=(k == KC - 1)),
                                    reads=[tok, ("xn", xo)], writes=[("ps", pq)], signal=(k == KC - 1))
                            if True:
                                P.op("act", lambda e, pq=pq, c=c, c0=c0, c1=c1, n=n, dstT=dstT: e.copy(
                                    out=dstT[:, c, c0:c1], in_=ps[pq][:, 0:n]),
                                    reads=[("ps", pq)], writes=[(which, c, b)])
                slot, tok = W.next([(full_slot, win_v[:, :, 1024:1536])])
                for b in half:
                    xo = XO[b]
                    ntile = 1 if b == 0 else 4
                    for j in range(ntile):
                        nt = 16 if b == 0 else 128
                        ti = 0 if b == 0 else 1 + (b - 1) * 4 + j
                        pq = nextps()
                        for k in range(KC):
                            P.op("pe", lambda e, k=k, slot=slot, pq=pq, xo=xo, j=j, nt=nt: e.matmul(
                                ps[pq][0:nt, 0:512], lhsT=xn[:, k, xo + j * 128:xo + j * 128 + nt],
                                rhs=slot[:, k * 512:(k + 1) * 512], start=(k == 0), stop=(k == KC - 1)),
                                reads=[tok, ("xn", xo)], writes=[("ps", pq)], signal=(k == KC - 1))
                        P.op("act", lambda e, pq=pq, nt=nt, ti=ti: e.copy(
                            out=av[0:nt, ti, :, 0:64], in_=ps[pq][0:nt, 0:512].rearrange("p (h d) -> p h d", h=8)),
                            reads=[("ps", pq)], writes=[("av", ti)])
                def sl256(slot):
                    return slot[:, 0:2048].rearrange("p (k f) -> p k f", k=8)
                slot, tok = W.next([
                    (lambda slot: sl256(slot)[:, :, 0:64], win_v[:, :, 2048:2112]),
                    (lambda slot: sl256(slot)[:, :, 64:128], win_v[:, :, 2048:2112]),
                    (lambda slot: sl256(slot)[:, :, 128:136], win_v[:, :, 2112:2120])])
                for b in half:
                    c0, c1 = BLKS[b]
                    n = c1 - c0
                    xo = XO[b]
                    pq = nextps()
                    for k in range(KC):
                        P.op("pe", lambda e, k=k, slot=slot, pq=pq, xo=xo, n=n: e.matmul(
                            ps[pq][:, 0:n], lhsT=slot[:, k * 256:k * 256 + 128],
                            rhs=xn[:, k, xo:xo + n], start=(k == 0), stop=(k == KC - 1)),
                            reads=[tok, ("xn", xo)], writes=[("ps", pq)], signal=(k == KC - 1))
                    P.op("act", lambda e, pq=pq, c0=c0, c1=c1, n=n: e.copy(out=ikT2[:, c0:c1], in_=ps[pq][:, 0:n]),
                         reads=[("ps", pq)], writes=[("ik", b)])
                    ntile = 1 if b == 0 else 4
                    for j in range(ntile):
                        nt = 16 if b == 0 else 128
                        ti = 0 if b == 0 else 1 + (b - 1) * 4 + j
                        pq = nextps()
                        for k in range(KC):
                            P.op("pe", lambda e, k=k, slot=slot, pq=pq, xo=xo, j=j, nt=nt: e.matmul(
                                ps[pq][0:nt, 0:8], lhsT=xn[:, k, xo + j * 128:xo + j * 128 + nt],
                                rhs=slot[:, k * 256 + 128:k * 256 + 136], start=(k == 0), stop=(k == KC - 1)),
                                reads=[tok, ("xn", xo)], writes=[("ps", pq)], signal=(k == KC - 1))
                        P.op("act", lambda e, pq=pq, nt=nt, ti=ti: e.activation(
                            out=iwabs[0:nt, ti, :], in_=ps[pq][0:nt, 0:8], func=AF.Abs, scale=IWS),
                            reads=[("ps", pq)], writes=[("iwabs", ti)])
                        P.op("act", lambda e, pq=pq, nt=nt, ti=ti: e.activation(
                            out=iwsgn[0:nt, ti, :], in_=ps[pq][0:nt, 0:8], func=AF.Sign),
                            reads=[("ps", pq)], writes=[("iwsgn", ti)])
            W.fence()
            P.barrier()

            ALLQ = [(w, c, bb) for w in ("aq", "ak", "iq") for c in range(4) for bb in range(5)] + [("ik", bb) for bb in range(5)]
            ecnt = [0]
            rcnt = [0]
            lo0 = bvec[:, 0:1]
            hi0 = bvec[:, 1:2]
            w0 = bvec[:, 2:3]
            mid = bvec[:, 3:4]
            cnt = bvec[:, 4:5]
            tt = bvec[:, 5:6]
            thr = bvec[:, 6:7]
            steps = bvec[:, 8:8 + NIT + 1]
            P.op("dve", lambda e: e.tensor_scalar(out=ident30k, in0=ident_f, scalar1=30000.0, scalar2=None, op0=ALU.mult),
                 reads=["identf", "lnv"], writes=["i30k", "lnv"])

            def tile_params(m):
                nq = 16 if m < 0 else 128
                q0 = 0 if m < 0 else 16 + 128 * m
                ti = 0 if m < 0 else 1 + m
                KR = 16 if m < 0 else 16 + 128 * (m + 1)
                return nq, q0, ti, KR

            def stageI(m):
                nq, q0, ti, KR = tile_params(m)
                score = score_b[ti % 2]
                sb_ = ti % 2
                kblocks = [(k0, min(k0 + 512, KR)) for k0 in range(0, KR, 512)]
                P.op("dve", lambda e: e.tensor_tensor(
                    out=Dh[0:nq, :, 0:nq], in0=ident_b[0:nq, 0:nq].unsqueeze(1).to_broadcast([nq, 8, nq]),
                    in1=iwsgn[0:nq, ti, :].unsqueeze(2).to_broadcast([nq, 8, nq]), op=ALU.mult),
                    reads=["identb", ("iwsgn", ti)], writes=["Dh"])
                for (k0, k1) in kblocks:
                    kw = k1 - k0
                    pend = []

                    def emit_logits(h, k0=k0, k1=k1, kw=kw, pend=pend):
                        pb = (h % 2) * 64
                        c = h // 2
                        pq = nextps()
                        P.op("pe", lambda e, pb=pb, c=c, pq=pq: e.matmul(
                            ps[pq][0:nq, 0:kw], lhsT=iqT[pb:pb + 64, c, q0:q0 + nq], rhs=ikT2[pb:pb + 64, k0:k1],
                            start=True, stop=True),
                            reads=ALLQ + [("iqya", ti)], writes=[("ps", pq)])
                        ri = rcnt[0] % 4
                        rcnt[0] += 1
                        rt = relub[ri]
                        P.op("act", lambda e, rt=rt, pq=pq, h=h: e.activation(
                            out=rt[0:nq, 0:kw], in_=ps[pq][0:nq, 0:kw], func=AF.Relu, scale=iwabs[0:nq, ti, h:h + 1]),
                            reads=[("ps", pq), ("iwabs", ti)], writes=[("relub", ri)])
                        pend.append((h, rt, ri))

                    def emit_diag(kw=kw, pend=pend):
                        h, rt, ri = pend.pop(0)
                        P.op("pe", lambda e, rt=rt, h=h: e.matmul(
                            ps[6][0:nq, 0:kw], lhsT=Dh[0:nq, h, 0:nq], rhs=rt[0:nq, 0:kw], start=(h == 0), stop=(h == 7)),
                            reads=[("relub", ri), "Dh"], writes=[("ps", 6)], signal=True)

                    emit_logits(0)
                    emit_logits(1)
                    for h in range(2, 8):
                        emit_logits(h)
                        emit_diag()
                    emit_diag()
                    emit_diag()
                    P.op("act", lambda e, kw=kw, k0=k0, k1=k1: e.copy(out=score[0:nq, k0:k1], in_=ps[6][0:nq, 0:kw]),
                         reads=[("ps", 6)], writes=[("score", sb_, k0)])

            def stageS(m):
                nq, q0, ti, KR = tile_params(m)
                score = score_b[ti % 2]
                sb_ = ti % 2
                mb = mbq[ti % 2]
                mbtok = ("mbq", ti % 2)
                kblocks = [(k0, min(k0 + 512, KR)) for k0 in range(0, KR, 512)]
                sc_tok = [("score", sb_, k0) for (k0, _) in kblocks]
                P.op("dve", lambda e: e.tensor_reduce(out=lo0[0:nq], in_=score[0:nq, 0:KR], op=ALU.min, axis=AX.X),
                     reads=sc_tok, writes=["lo0"])
                if KR > 256:
                    P.op("dve", lambda e: e.tensor_reduce(out=hi0[0:nq], in_=score[0:nq, 0:KR], op=ALU.max, axis=AX.X),
                         reads=sc_tok, writes=["hi0"])
                if m >= 0:
                    P.op("dve", lambda e: e.memset(score[0:64, KR - 64:KR], -1e30), reads=["hi0", "lo0"],
                         writes=sc_tok)
                if KR > 256:
                    P.op("dve", lambda e: e.tensor_tensor(out=w0[0:nq], in0=hi0[0:nq], in1=lo0[0:nq], op=ALU.subtract),
                         reads=["hi0", "lo0"], writes=["w0"])
                    P.op("dve", lambda e: e.tensor_scalar(out=tt[0:nq], in0=w0[0:nq], scalar1=0.001, scalar2=1e-5,
                                                          op0=ALU.mult, op1=ALU.add), reads=["w0"], writes=["tt"])
                    P.op("dve", lambda e: e.tensor_tensor(out=lo0[0:nq], in0=lo0[0:nq], in1=tt[0:nq], op=ALU.subtract),
                         reads=["tt", "lo0"], writes=["lo0"])
                    P.op("dve", lambda e: e.tensor_tensor(out=w0[0:nq], in0=hi0[0:nq], in1=lo0[0:nq], op=ALU.subtract),
                         reads=["hi0", "lo0"], writes=["w0"])
                    P.op("dve", lambda e: e.tensor_scalar(out=steps[0:nq, :], in0=ctab[0:nq, 0:NIT + 1], scalar1=w0[0:nq],
                                                          scalar2=None, op0=ALU.mult),
                         reads=["w0", "ctab"], writes=["steps"])
                    P.op("dve", lambda e: e.tensor_tensor(out=mid[0:nq], in0=lo0[0:nq], in1=steps[0:nq, 0:1], op=ALU.add),
                         reads=["lo0", "steps"], writes=["mid"])
                    for it in range(NIT):
                        P.op("dve", lambda e: e.tensor_scalar(
                            out=junk[0:nq, 0:KR], in0=score[0:nq, 0:KR], scalar1=mid[0:nq], scalar2=None,
                            op0=ALU.is_ge, op1=ALU.add, accum_out=cnt[0:nq]),
                            reads=sc_tok + ["mid"], writes=["cnt", "junk"])
                        P.op("dve", lambda e: e.tensor_scalar(out=tt[0:nq], in0=cnt[0:nq], scalar1=255.5, scalar2=0.5,
                                                              op0=ALU.is_gt, op1=ALU.subtract),
                             reads=["cnt"], writes=["tt"])
                        P.op("dve", lambda e, it=it: e.scalar_tensor_tensor(
                            out=mid[0:nq], in0=tt[0:nq], scalar=steps[0:nq, it:it + 1], op0=ALU.mult, in1=mid[0:nq], op1=ALU.add),
                            reads=["tt", "steps", "mid"], writes=["mid"])
                    P.op("dve", lambda e: e.tensor_tensor(out=thr[0:nq], in0=mid[0:nq], in1=steps[0:nq, NIT:NIT + 1],
                                                          op=ALU.subtract),
                         reads=["mid", "steps"], writes=["thr"])
                else:
                    P.op("dve", lambda e: e.tensor_scalar(out=thr[0:nq], in0=lo0[0:nq], scalar1=1.0, scalar2=None,
                                                          op0=ALU.subtract), reads=["lo0"], writes=["thr"])
                P.op("dve", lambda e: e.tensor_scalar(
                    out=mb[0:nq, 0:KR], in0=score[0:nq, 0:KR], scalar1=thr[0:nq], scalar2=-1.0, op0=ALU.is_ge, op1=ALU.add),
                    reads=sc_tok + ["thr"], writes=[mbtok])

            def stageB(m):
                nq, q0, ti, KR = tile_params(m)
                mb = mbq[ti % 2]
                mbtok = ("mbq", ti % 2)
                ktiles = [(0, 16, 0)] + ([(1 + j, 128, 16 + 128 * j) for j in range(m + 1)] if m >= 0 else [])
                groups = [[ktiles[0]]] + [ktiles[1:][i:i + 4] for i in range(0, len(ktiles) - 1, 4)]
                items = []
                for h in range(8):
                    pb = (h % 2) * 64
                    c = h // 2
                    po = 4 + h // 4
                    hc = h % 4
                    nmm = len(ktiles)
                    imm = 0
                    for grp in groups:
                        ei = ecnt[0] % 2
                        ecnt[0] += 1
                        nkmax = grp[0][1]
                        ng = len(grp)
                        E = Eb[ei]
                        st_ = {}

                        def front(grp=grp, pb=pb, c=c, ei=ei, nkmax=nkmax, ng=ng, E=E, st_=st_):
                            pq = nextps()
                            st_["pq"] = pq
                            for gi, (kti, nk, kc0) in enumerate(grp):
                                P.op("pe", lambda e, gi=gi, nk=nk, kc0=kc0, pb=pb, c=c, pq=pq: e.matmul(
                                    ps[pq][0:nk, gi * 128:gi * 128 + nq], lhsT=akT[pb:pb + 64, c, kc0:kc0 + nk],
                                    rhs=aqT[pb:pb + 64, c, q0:q0 + nq], start=True, stop=False),
                                    reads=ALLQ, writes=[("ps", pq)], signal=False)
                                P.op("pe", lambda e, gi=gi, nk=nk, kc0=kc0, pq=pq: e.matmul(
                                    ps[pq][0:nk, gi * 128:gi * 128 + nq], lhsT=mb[0:nq, kc0:kc0 + nk],
                                    rhs=ident30k[0:nq, 0:nq], start=False, stop=True),
                                    reads=[mbtok, "i30k"], writes=[("ps", pq)], signal=(gi == ng - 1))
                            P.op("act", lambda e, E=E, pq=pq, nkmax=nkmax, ng=ng: e.activation(
                                out=E[0:nkmax, 0:ng * 128].rearrange("p (j q) -> p j q", q=128)[:, :, 0:nq],
                                in_=ps[pq][0:nkmax, 0:ng * 128].rearrange("p (j q) -> p j q", q=128)[:, :, 0:nq],
                                func=AF.Exp, scale=0.125),
                                reads=[("ps", pq)], writes=[("E", ei)])

                        def back(grp=grp, ei=ei, ng=ng, E=E, po=po, hc=hc, h=h, imm0=imm, nmm=nmm):
                            imm_ = imm0
                            for gi, (kti, nk, kc0) in enumerate(grp):
                                P.op("pe", lambda e, gi=gi, nk=nk, kti=kti, E=E, po=po, hc=hc, h=h, imm_=imm_, nmm=nmm: e.matmul(
                                    ps[po][0:nq, hc * 65:hc * 65 + 65], lhsT=E[0:nk, gi * 128:gi * 128 + nq],
                                    rhs=av[0:nk, kti, h, :], start=(imm_ == 0), stop=(imm_ == nmm - 1)),
                                    reads=[("E", ei), ("av", kti), "av1"], writes=[("ps", po)],
                                    signal=(imm_ == nmm - 1 or gi == ng - 1))
                                imm_ += 1
                        imm += len(grp)
                        items.append((front, back))
                pipeline(items)

            def stageB_tail(m):
                nq, q0, ti, KR = tile_params(m)
                for hh in range(2):
                    po = 4 + hh
                    pv = ps[po][0:nq, 0:260].rearrange("p (c e) -> p c e", e=65)
                    P.op("dve", lambda e, pv=pv, hh=hh: e.reciprocal(out=rec[0:nq, hh * 4:hh * 4 + 4], in_=pv[:, :, 64]),
                         reads=[("ps", po)], writes=[("rec", hh)])
                    P.op("dve", lambda e, pv=pv, hh=hh: e.tensor_tensor(
                        out=yatok[0:nq, hh * 256:(hh + 1) * 256].rearrange("p (c d) -> p c d", c=4),
                        in0=pv[:, :, 0:64], in1=rec[0:nq, hh * 4:hh * 4 + 4].unsqueeze(2).to_broadcast([nq, 4, 64]),
                        op=ALU.mult),
                        reads=[("ps", po), ("rec", hh)], writes=[("yatok", hh)])
                for c in range(4):
                    P.op("pe", lambda e, c=c: e.transpose(
                        out=ps7b[:, c * 128:c * 128 + nq], in_=yatok[0:nq, c * 128:(c + 1) * 128],
                        identity=ident_b[0:nq, 0:nq]),
                        reads=[("yatok", 0), ("yatok", 1), "identb"], writes=[("ps", 7)], signal=(c == 3))
                P.op("act", lambda e: e.copy(
                    out=iqT[:, :, q0:q0 + nq], in_=ps7b[:, 0:512].rearrange("p (c q) -> p c q", c=4)[:, :, 0:nq]),
                    reads=[("ps", 7)], writes=[("iqya", ti)])

            if True:
                stageI(-1)
                stageI(0)
                stageS(-1)
                for m in range(-1, 16):
                    if m + 2 <= 15:
                        stageI(m + 2)
                    if m - 1 >= -1:
                        stageB_tail(m - 1)
                    if m + 1 <= 15:
                        stageS(m + 1)
                    stageB(m)
                stageB_tail(15)
            P.barrier()
            W.release()

            P.dma("sp", "rcst0", DT, dt_d, writes=["DT"])
            P.dma("sp", "rcst1", qdecT, qdect_d, writes=["qdecT"])
            P.dma("sp", "rcst2", rc, rc_d, writes=["rc"])
            P.dma("pool", "rcst3", perm_sb, perm_d, writes=["perm"])
            pass
            wo_v = ev_out[0].rearrange("(j p) d -> p j d", p=128)
            acnt = [0]
            first_tile = [True]
            def ret_block(b):
                c0, c1 = BLKS[b]
                n = c1 - c0
                Ln = 16 if b == 0 else 128
                ntile = 1 if b == 0 else 4
                P.dma("sp", "cs", cs_sb[:, :, 0:n], cs_d[:, :, c0:c1].rearrange("a p t -> p a t"), writes=["cs"])
                norm_block(b, 16 + 0, 0, dst=xnR, dtok="xnR")
                for which, col0, raw, rot in (("bq", 2120, bqraw, bqr), ("bk", 2632, bkraw, bkr)):
                    slot, tok = W.next([(full_slot, win_v[:, :, col0:col0 + 512])])
                    items = []
                    for c in range(4):
                        def front(c=c, slot=slot, tok=tok, raw=raw, which=which):
                            pq = nextps()
                            for k in range(KC):
                                P.op("pe", lambda e, k=k, c=c, slot=slot, pq=pq: e.matmul(
                                    ps[pq][:, 0:n], lhsT=slot[:, k * 512 + c * 128:k * 512 + c * 128 + 128],
                                    rhs=xnR[:, k, 0:n], start=(k == 0), stop=(k == KC - 1)),
                                    reads=[tok, "xnR"], writes=[("ps", pq)], signal=(k == KC - 1))
                            P.op("act", lambda e, pq=pq, c=c, raw=raw: e.copy(out=raw[:, c, 0:n], in_=ps[pq][:, 0:n]),
                                 reads=[("ps", pq)], writes=[(which + "raw", c)])

                        def back(c=c, raw=raw, rot=rot, which=which):
                            pq2 = nextps()
                            P.op("pe", lambda e, pq2=pq2, c=c, raw=raw: e.matmul(
                                ps[pq2][:, 0:n], lhsT=perm_sb, rhs=raw[:, c, 0:n], start=True, stop=True),
                                reads=[(which + "raw", c), "perm"], writes=[("ps", pq2)])
                            P.op("dve", lambda e, c=c, raw=raw: e.tensor_tensor(
                                out=lnv[:, 0:n], in0=raw[:, c, 0:n], in1=cs_sb[:, 0, 0:n], op=ALU.mult),
                                reads=[(which + "raw", c), "cs"], writes=["lnv"])
                            P.op("dve", lambda e, pq2=pq2: e.tensor_tensor(
                                out=rstd[:, 0:n], in0=ps[pq2][:, 0:n], in1=cs_sb[:, 1, 0:n], op=ALU.mult),
                                reads=[("ps", pq2), "cs"], writes=["rstd"])
                            P.op("dve", lambda e, c=c, rot=rot: e.tensor_tensor(
                                out=rot[:, c, 0:n], in0=lnv[:, 0:n], in1=rstd[:, 0:n], op=ALU.add),
                                reads=["lnv", "rstd"], writes=[(which + "r", c)])
                        items.append((front, back))
                    pipeline(items)
                for which, col0, dstt in (("bv", 3144, bv_tok), ("bg", 4168, bgs_tok)):
                    for hf in range(2):
                        slot, tok = W.next([(full_slot, win_v[:, :, col0 + hf * 512:col0 + (hf + 1) * 512])])
                        for j in range(ntile):
                            pq = nextps()
                            for k in range(KC):
                                P.op("pe", lambda e, k=k, slot=slot, pq=pq, j=j: e.matmul(
                                    ps[pq][0:Ln, 0:512], lhsT=xnR[:, k, j * 128:j * 128 + Ln],
                                    rhs=slot[:, k * 512:(k + 1) * 512], start=(k == 0), stop=(k == KC - 1)),
                                    reads=[tok, "xnR"], writes=[("ps", pq)], signal=(k == KC - 1))
                            if which == "bv":
                                P.op("act", lambda e, pq=pq, j=j, hf=hf, dstt=dstt: e.copy(
                                    out=dstt[0:Ln, j, hf * 512:(hf + 1) * 512], in_=ps[pq][0:Ln, 0:512]),
                                    reads=[("ps", pq)], writes=[(which, j, hf)])
                            else:
                                P.op("act", lambda e, pq=pq, j=j, hf=hf, dstt=dstt: e.activation(
                                    out=dstt[0:Ln, j, hf * 512:(hf + 1) * 512], in_=ps[pq][0:Ln, 0:512], func=AF.Silu),
                                    reads=[("ps", pq)], writes=[(which, j, hf)])
                kdc = 8 if Ln == 16 else 0
                cdc = 20 if Ln == 16 else 16
                p2 = [0]

                def nextps2():
                    p2[0] += 1
                    return p2[0] % 2

                def tile_heads(j):
                    lo_ = j * 128
                    pob = 4 if j % 2 == 0 else 2
                    for c in range(4):
                        P.op("dve", lambda e, c=c, lo_=lo_: e.tensor_tensor(
                            out=bqd[:, c, lo_:lo_ + Ln], in0=bqr[:, c, lo_:lo_ + Ln], in1=qdecT[:, c, 0:Ln], op=ALU.mult),
                            reads=[("bqr", c), "qdecT"], writes=[("bqd", c)])
                    for c in range(4):
                        P.op("pe", lambda e, c=c, lo_=lo_: e.transpose(
                            out=ps7b[0:Ln, c * 128:(c + 1) * 128], in_=bkr[:, c, lo_:lo_ + Ln], identity=ident_b),
                            reads=[("bkr", c), "identb"], writes=[("ps", 7)], signal=(c == 3))
                    P.op("dve", lambda e: e.tensor_tensor(
                        out=kd_tok[0:Ln, :].rearrange("p (h d) -> p h d", h=8),
                        in0=ps7b[0:Ln, 0:512].rearrange("p (h d) -> p h d", h=8),
                        in1=rc[0:Ln, kdc:kdc + 8].unsqueeze(2).to_broadcast([Ln, 8, 64]), op=ALU.mult),
                        reads=[("ps", 7), "rc"], writes=["kd"])
                    items = []
                    ft = first_tile[0]
                    for h in range(8):
                        pb = (h % 2) * 64
                        c = h // 2
                        po = pob + h // 4
                        hc = h % 4
                        ai = acnt[0] % 2
                        acnt[0] += 1
                        AT = ATb[ai]

                        def front(pb=pb, c=c, lo_=lo_, ai=ai, AT=AT, h=h):
                            pq = nextps2()
                            P.op("pe", lambda e, pb=pb, c=c, pq=pq, lo_=lo_: e.matmul(
                                ps[pq][0:Ln, 0:Ln], lhsT=bkr[pb:pb + 64, c, lo_:lo_ + Ln], rhs=bqr[pb:pb + 64, c, lo_:lo_ + Ln],
                                start=True, stop=True),
                                reads=[("bkr", c), ("bqr", c)], writes=[("ps", pq)])
                            P.op("dve", lambda e, pq=pq, AT=AT, h=h: e.tensor_tensor(
                                out=AT[0:Ln, 0:Ln], in0=ps[pq][0:Ln, 0:Ln], in1=DT[0:Ln, h, 0:Ln], op=ALU.mult),
                                reads=[("ps", pq), "DT"], writes=[("AT", ai)])

                        def back(pb=pb, c=c, lo_=lo_, ai=ai, AT=AT, h=h, po=po, hc=hc, j=j, ft=ft):
                            P.op("pe", lambda e, AT=AT, po=po, hc=hc, h=h, j=j, ft=ft: e.matmul(
                                ps[po][0:Ln, hc * 128:(hc + 1) * 128], lhsT=AT[0:Ln, 0:Ln], rhs=bv_tok[0:Ln, j, h * 128:(h + 1) * 128],
                                start=True, stop=ft),
                                reads=[("AT", ai), ("bv", j, 0), ("bv", j, 1)], writes=[("ps", po)], signal=ft)
                            if not ft:
                                P.op("pe", lambda e, pb=pb, c=c, po=po, hc=hc, lo_=lo_: e.matmul(
                                    ps[po][0:Ln, hc * 128:(hc + 1) * 128], lhsT=bqd[pb:pb + 64, c, lo_:lo_ + Ln],
                                    rhs=S_bf[pb:pb + 64, c, :], start=False, stop=True),
                                    reads=[("bqd", c), "Sbf"], writes=[("ps", po)])
                            P.op("pe", lambda e, pb=pb, c=c, h=h, j=j: e.matmul(
                                ps[6][pb:pb + 64, c * 128:(c + 1) * 128], lhsT=kd_tok[0:Ln, h * 64:(h + 1) * 64],
                                rhs=bv_tok[0:Ln, j, h * 128:(h + 1) * 128], start=True, stop=True),
                                reads=["kd", ("bv", j, 0), ("bv", j, 1)], writes=[("ps", 6)], signal=(h == 7))
                        items.append((front, back))
                    pipeline(items)
                    for c in range(4):
                        if first_tile[0]:
                            P.op("dve", lambda e, c=c: e.tensor_copy(out=S_sb[:, c, :], in_=ps[6][:, c * 128:(c + 1) * 128]),
                                 reads=[("ps", 6)], writes=[("S", c)])
                        else:
                            P.op("dve", lambda e, c=c: e.scalar_tensor_tensor(
                                out=S_sb[:, c, :], in0=S_sb[:, c, :], scalar=rc[:, cdc + c:cdc + c + 1], op0=ALU.mult,
                                in1=ps[6][:, c * 128:(c + 1) * 128], op1=ALU.add),
                                reads=[("ps", 6), ("S", c), "rc"], writes=[("S", c)])
                    P.op("act", lambda e: e.copy(out=S_bf, in_=S_sb), reads=[("S", c) for c in range(4)], writes=["Sbf"])
                    if first_tile[0] and DEBUG:
                        P.op("act", lambda e: e.copy(out=sil[1], in_=ps[6]), reads=[("ps", 6)], writes=[("sil", 1)])
                        dump("ps6_first", sil[1], [("sil", 1)])
                    if first_tile[0]:
                        dump("S_first", S_sb.rearrange("p c n -> p (c n)"), [("S", c_) for c_ in range(4)])
                        dump("kd_first", kd_tok, ["kd"])
                        dump("bv_first", bv_tok[:, 0, :], [("bv", 0, 0), ("bv", 0, 1)])
                    first_tile[0] = False
                def tile_out(j):
                    lo_ = j * 128
                    pob = 4 if j % 2 == 0 else 2
                    for hh in range(2):
                        po = pob + hh
                        P.op("act", lambda e, po=po, hh=hh: e.activation(out=sil[hh][0:Ln, :], in_=ps[po][0:Ln, :], func=AF.Square),
                             reads=[("ps", po)], writes=[("sil", hh)])
                        P.op("dve", lambda e, hh=hh: e.tensor_reduce(
                            out=den[0:Ln, hh * 4:hh * 4 + 4], in_=sil[hh][0:Ln, :].rearrange("p (h e) -> p h e", h=4),
                            axis=AX.X, op=ALU.add),
                            reads=[("sil", hh)], writes=[("den", hh)])
                    P.op("act", lambda e: e.activation(out=rec[0:Ln, :], in_=den[0:Ln, :], func=AF.Ln, scale=1.0 / 128, bias=eps_sb[0:Ln]),
                         reads=[("den", 0), ("den", 1), "eps"], writes=[("rec", 0), ("rec", 1)])
                    P.op("act", lambda e: e.activation(out=rec[0:Ln, :], in_=rec[0:Ln, :], func=AF.Exp, scale=-0.5),
                         reads=[("rec", 0), ("rec", 1)], writes=[("rec", 0), ("rec", 1)])
                    for hh in range(2):
                        po = pob + hh
                        P.op("dve", lambda e, po=po, hh=hh: e.tensor_tensor(
                            out=otmp[0:Ln, hh * 512:(hh + 1) * 512].rearrange("p (h e) -> p h e", h=4),
                            in0=ps[po][0:Ln, :].rearrange("p (h e) -> p h e", h=4),
                            in1=rec[0:Ln, hh * 4:hh * 4 + 4].unsqueeze(2).to_broadcast([Ln, 4, 128]), op=ALU.mult),
                            reads=[("ps", po), ("rec", hh)], writes=["sq"], signal=True)
                    P.op("dve", lambda e, j=j: e.tensor_tensor(
                        out=yb_tok[0:Ln, :], in0=otmp[0:Ln, :], in1=bgs_tok[0:Ln, j, :], op=ALU.mult),
                        reads=["sq", ("bg", j, 0), ("bg", j, 1)], writes=["ybtok"])
                def tile_out_tail(j):
                    lo_ = j * 128
                    for k in range(8):
                        P.op("pe", lambda e, k=k: e.transpose(
                            out=ps7b[:, k * 128:k * 128 + Ln], in_=yb_tok[0:Ln, k * 128:(k + 1) * 128],
                            identity=ident_b[0:Ln, 0:Ln]),
                            reads=["ybtok", "identb"], writes=[("ps", 7)], signal=(k == 7))
                    P.op("act", lambda e, lo_=lo_: e.copy(
                        out=ybT_blk[:, :, lo_:lo_ + Ln], in_=ps7b.rearrange("p (k q) -> p k q", k=8)[:, :, 0:Ln]),
                        reads=[("ps", 7)], writes=[("ybT", j)])

                for j in range(ntile):
                    tile_heads(j)
                    if j >= 2:
                        tile_out_tail(j - 2)
                    if j >= 1:
                        tile_out(j - 1)
                if ntile >= 2:
                    tile_out_tail(ntile - 2)
                tile_out(ntile - 1)
                tile_out_tail(ntile - 1)
                for dh in range(2):
                    slot, tok = W.next([(lambda slot: slot[:, 0:6144].rearrange("p (j f) -> p j f", j=12),
                                         wo_v[:, :, dh * 512:(dh + 1) * 512])])
                    for dc in range(4):
                        pq = nextps()
                        dch = dh * 4 + dc
                        js = ([0, 1, 2, 3] if EV_DSA else []) + ([4 + q_ for q_ in range(8)] if EV_RET else [])
                        for ji, j in enumerate(js):
                            rhs = iqT[:, j, c0:c1] if j < 4 else ybT_blk[:, j - 4, 0:n]
                            rtok = [("iqya", t_) for t_ in range(17)] if j < 4 else [("ybT", q_) for q_ in range(ntile)]
                            P.op("pe", lambda e, j=j, ji=ji, dc=dc, slot=slot, pq=pq, rhs=rhs, n=n, nj=len(js): e.matmul(
                                ps[pq][:, 0:n], lhsT=slot[:, j * 512 + dc * 128:j * 512 + dc * 128 + 128], rhs=rhs,
                                start=(ji == 0), stop=(ji == nj - 1)),
                                reads=[tok] + rtok, writes=[("ps", pq)], signal=(ji == len(js) - 1))
                        P.op("dve", lambda e, pq=pq, n=n, dch=dch, c0=c0, c1=c1: e.tensor_tensor(
                            out=hT[:, dch, c0:c1], in0=ps[pq][:, 0:n], in1=hT[:, dch, c0:c1], op=ALU.add),
                            reads=[("ps", pq), ("h", b, dch)], writes=[("h", b, dch)])
            for b_ in range(5):
                ret_block(b_)
            P.barrier()
            dump("bqd", bqd.rearrange("p c n -> p (c n)"), [])
            dump("DT", DT.rearrange("p c n -> p (c n)"), [])
            dump("qdecT", qdecT.rearrange("p c n -> p (c n)"), [])
            dump("rc", rc, [])
            dump("AT0", ATb[0], [])
            dump("bqraw", bqraw.rearrange("p c n -> p (c n)"), [])
            dump("bqr", bqr.rearrange("p c n -> p (c n)"), [])
            dump("bkr", bkr.rearrange("p c n -> p (c n)"), [])
            dump("S_sb", S_sb.rearrange("p c n -> p (c n)"), [])
            dump("ybT", ybT_blk.rearrange("p c n -> p (c n)"), [])
            dump("rec", rec, [])
            dump("den", den, [])
            dump("otmp", otmp, [])
            dump("kd", kd_tok, [])
            dump("ybtok", yb_tok, [])

            P.barrier()

        for layer in range(2):
            if f"f1_{layer}" in stages:
                ffn(f1_in[layer], f1_out[layer], 0 + layer * 8)
            if f"mix_{layer}" in stages and layer == 0:
                P.barrier()
                even_mixer()
            if f"mix_{layer}" in stages and layer == 1:
                P.barrier()
                odd_mixer()
            if f"f2_{layer}" in stages:
                ffn(f2_in[layer], f2_out[layer], 32 + layer * 8)
        P.barrier()

        for m in range(16):
            col0 = NM + m * 128
            buf = io[m % 2]
            for half in range(2):
                pt = ps[6 + half]
                for kk in range(4):
                    k = half * 4 + kk
                    P.op("pe", lambda e, k=k, kk=kk, pt=pt, col0=col0: e.transpose(
                        out=pt[:, kk * 128:(kk + 1) * 128], in_=hT[:, k, col0:col0 + 128], identity=ident_f),
                        reads=["identf"], writes=[("ps", 6 + half)], signal=(kk == 3))
                if half == 0:
                    P.op("dve", lambda e, pt=pt, buf=buf: e.tensor_copy(out=buf[:, 0:512], in_=pt),
                         reads=[("ps", 6)], writes=[("io", m % 2, 0)])
                else:
                    P.op("act", lambda e, pt=pt, buf=buf: e.copy(out=buf[:, 512:1024], in_=pt),
                         reads=[("ps", 7)], writes=[("io", m % 2, 1)])
            P.dma("sp", f"io{m%2}", out[m * 128:(m + 1) * 128, :], buf,
                  reads=[("io", m % 2, 0), ("io", m % 2, 1)])
        P.barrier()

    eps_sb = nc.alloc_sbuf_tensor("eps_sb", [128, 1], F32).ap()

    def prog2():
        P.op("dve", lambda e: e.memset(eps_sb, EPS), writes=["eps"])
        program()

    P.plan = True
    prog2()
    P.plan = False
    P.reset()
    W.start_real()
    prog2()
    global LAST_PROG
    LAST_PROG = P
    P.emit()
    return nc


LAST_PROG = None


def _prep_common(inputs):
    vec = np.zeros((128, 64), np.float32)
    vec[:, 48] = np.tile(inputs["od_c_q_norm"][0], 2)
    vec[:, 49] = np.tile(inputs["od_c_k_norm"][0], 2)
    vec[:, 50] = np.tile(inputs["ev_a_q_norm"][0], 2)
    vec[:, 51] = np.tile(inputs["ev_a_k_norm"][0], 2)
    vec[:, 52:56] = inputs["od_d_scale"][0].reshape(4, 128).T
    vec[:, 56:64] = np.broadcast_to(inputs["od_c_sinks"][0][None, :], (128, 8))
    invc = np.zeros((128, 64), np.float32)
    for g, w in enumerate((2, 4, 8, 16)):
        invc[:, g * 16:(g + 1) * 16] = 1.0 / np.minimum(np.arange(16) + 1, w)[None, :]
    for li in range(2):
        vec[:, 0 + li * 8:8 + li * 8] = inputs["ffn1_norm"][li].reshape(8, 128).T
        vec[:, 16 + li * 8:24 + li * 8] = inputs["mix_norm"][li].reshape(8, 128).T
        vec[:, 32 + li * 8:40 + li * 8] = inputs["ffn2_norm"][li].reshape(8, 128).T
    perm = np.concatenate([np.arange((c + 4 * hh) * 64, (c + 4 * hh) * 64 + 64) for c in range(4) for hh in range(2)])
    od_in_p = np.ascontiguousarray(inputs["od_w_in"]).copy()
    od_in_p[:, :, 0:512] = inputs["od_w_in"][:, :, perm]
    od_out_p = np.ascontiguousarray(inputs["od_w_out"]).copy()
    od_out_p[:, 0:512, :] = inputs["od_w_out"][:, perm, :]
    tpos = np.arange(T, dtype=np.float64)
    inv = 1.0 / (10000.0 ** (np.arange(0, 64, 2, dtype=np.float64) / 64.0))
    pidx = np.arange(128)
    ang = tpos[None, :] * inv[pidx % 32][:, None]
    sgn = np.where((pidx % 64) < 32, -1.0, 1.0)[:, None]
    cs_tab = np.stack([np.cos(ang), np.sin(ang) * sgn]).astype(np.float32)
    perm_tab = np.zeros((128, 128), np.float32)
    for m_ in range(128):
        perm_tab[m_ + 32 if (m_ % 64) < 32 else m_ - 32, m_] = 1.0
    gam = 1.0 - 2.0 ** (-5.0 - np.arange(8, dtype=np.float64))
    ii = np.arange(128, dtype=np.float64)
    rel = ii[None, :] - ii[:, None]
    dt_tab = np.zeros((128, 8, 128), np.float64)
    for h_ in range(8):
        dt_tab[:, h_, :] = np.where(rel >= 0, 0.125 * gam[h_] ** np.maximum(rel, 0.0), 0.0)
    qdect = np.zeros((128, 4, 128), np.float64)
    for c_ in range(4):
        for p_ in range(128):
            qdect[p_, c_, :] = gam[2 * c_ + p_ // 64] ** (ii + 1.0)
    rc_tab = np.zeros((128, 64), np.float64)
    for h_ in range(8):
        rc_tab[:, h_] = 0.125 * gam[h_] ** np.maximum(127.0 - ii, 0.0)
        rc_tab[:, 8 + h_] = 0.125 * gam[h_] ** np.maximum(15.0 - ii, 0.0)
    for c_ in range(4):
        rc_tab[:, 16 + c_] = gam[2 * c_ + pidx // 64] ** 128.0
        rc_tab[:, 20 + c_] = gam[2 * c_ + pidx // 64] ** 16.0
    com = {
        "meta_tokens": np.ascontiguousarray(inputs["meta_tokens"], np.float32),
        "ffn1_w_in": inputs["ffn1_w_in"], "ffn1_w_out": inputs["ffn1_w_out"],
        "ffn2_w_in": inputs["ffn2_w_in"], "ffn2_w_out": inputs["ffn2_w_out"],
        "vecs": vec, "ident": np.eye(128, dtype=np.float32), "invcnt": invc,
        "ctab": np.broadcast_to((0.5 ** (np.arange(32) + 1)).astype(np.float32)[None, :], (128, 32)).copy(),
        "ev_w_in": inputs["ev_w_in"], "ev_w_out": inputs["ev_w_out"],
        "cs_tab": cs_tab, "perm_tab": perm_tab, "dt_tab": dt_tab.astype(np.float32),
        "qdect_tab": qdect.astype(np.float32), "rc_tab": rc_tab.astype(np.float32),
        "od_w_in": od_in_p, "od_w_out": od_out_p, "od_d_mix": inputs["od_d_mix"],
    }
    return com


def kernel(**inputs):
    inputs = {k: np.asarray(v) for k, v in inputs.items()}
    nc = build()
    com = _prep_common(inputs)
    B = inputs["x"].shape[0]
    in_maps = []
    for b in range(B):
        m = dict(com)
        m["x"] = np.ascontiguousarray(inputs["x"][b])
        in_maps.append(m)
    res = run_bass_kernel_spmd(nc, in_maps, core_ids=list(range(B)))
    return np.stack([r["out"] for r in res.results], axis=0).astype(np.float32)
```
